# Optimizing a Trainium2 kernel written in Bass

```python
import jax, jax.numpy as jnp
from jax import lax
import numpy as np

D_MODEL = 1024
BATCH = 32
SEQ = 2048
DEPTH = 1
DEC_BATCH = 8
DEC_SEQ = 32
PAST_LEN = 1024

CHUNK = 64
C_CONV = D_MODEL // 2
CONV_WIDTH = 31
N_HEADS = 8
HEAD_DIM = 64
D_ATTN = N_HEADS * HEAD_DIM
D_FF = 4 * D_MODEL
Q_BLOCK = 128
LN_EPS = 1e-5
ATTN_SCALE = HEAD_DIM ** -0.5
NEG_INF = -1e30
DN_ALPHA = (2 * DEPTH) ** 0.25
DN_BETA = (8 * DEPTH) ** -0.25

OFF_GLU = 0
OFF_Q = OFF_GLU + 2 * C_CONV
OFF_K = OFF_Q + D_ATTN
OFF_V = OFF_K + D_ATTN
OFF_F = OFF_V + D_ATTN
OFF_G = OFF_F + N_HEADS
D_IN = OFF_G + 2 * D_MODEL

kernel_name = "conformer_conv_fox_gated_hybrid_step"


def _layer_norm(x, g, b):
    xf = x.astype(jnp.float32)
    mu = jnp.mean(xf, axis=-1, keepdims=True)
    var = jnp.mean(jnp.square(xf - mu), axis=-1, keepdims=True)
    y = (xf - mu) * lax.rsqrt(var + LN_EPS) * g.astype(jnp.float32) + b.astype(jnp.float32)
    return y.astype(x.dtype)


def _in_proj(x, w_in, b_f):
    bsz, t, _ = x.shape
    p = jnp.einsum('btd,de->bte', x, w_in)
    u = p[..., OFF_GLU:OFF_GLU + C_CONV] * jax.nn.sigmoid(p[..., OFF_GLU + C_CONV:OFF_Q])
    q = p[..., OFF_Q:OFF_K].reshape(bsz, t, N_HEADS, HEAD_DIM)
    k = p[..., OFF_K:OFF_V].reshape(bsz, t, N_HEADS, HEAD_DIM)
    v = p[..., OFF_V:OFF_F].reshape(bsz, t, N_HEADS, HEAD_DIM)
    logf = jax.nn.log_sigmoid(p[..., OFF_F:OFF_G].astype(jnp.float32) + b_f.astype(jnp.float32))
    g_conv = jax.nn.sigmoid(p[..., OFF_G:OFF_G + D_MODEL])
    g_attn = jax.nn.sigmoid(p[..., OFF_G + D_MODEL:D_IN])
    return u, q, k, v, logf, g_conv, g_attn


def _conv_branch(u_ext, w_dw, b_dw, ln_g, ln_b, w_out):
    y = lax.conv_general_dilated(
        u_ext, w_dw[:, None, :].astype(u_ext.dtype), window_strides=(1,), padding='VALID',
        dimension_numbers=('NWC', 'WIO', 'NWC'), feature_group_count=C_CONV)
    y = jax.nn.silu(_layer_norm(y + b_dw, ln_g, ln_b))
    return jnp.einsum('btc,cd->btd', y, w_out)


def _fox_attend(q, k, v, cq, ck, q_pos, k_pos):
    s = jnp.einsum('bqhd,bkhd->bhqk', q, k).astype(jnp.float32) * ATTN_SCALE
    bias = jnp.swapaxes(cq, 1, 2)[:, :, :, None] - jnp.swapaxes(ck, 1, 2)[:, :, None, :]
    mask = k_pos[None, :] <= q_pos[:, None]
    s = jnp.where(mask, s + bias, NEG_INF)
    p = jax.nn.softmax(s, axis=-1)
    return jnp.einsum('bhqk,bkhd->bqhd', p.astype(v.dtype), v)


def _fox_prompt(q, k, v, logf):
    bsz, t = q.shape[:2]
    c = jnp.cumsum(logf, axis=1)
    k_pos = jnp.arange(t)

    def block(i):
        start = i * Q_BLOCK
        qb = lax.dynamic_slice_in_dim(q, start, Q_BLOCK, axis=1)
        cb = lax.dynamic_slice_in_dim(c, start, Q_BLOCK, axis=1)
        return _fox_attend(qb, k, v, cb, c, start + jnp.arange(Q_BLOCK), k_pos)

    o = lax.map(block, jnp.arange(t // Q_BLOCK))
    return jnp.moveaxis(o, 0, 1).reshape(bsz, t, D_ATTN)


def _fox_sample(q, k, v, logf, cache_k, cache_v, cache_logf):
    bsz, t = q.shape[:2]
    past = cache_k.shape[1]
    k_all = jnp.concatenate([cache_k, k], axis=1)
    v_all = jnp.concatenate([cache_v, v], axis=1)
    c = jnp.cumsum(jnp.concatenate([cache_logf.astype(jnp.float32), logf], axis=1), axis=1)
    o = _fox_attend(q, k_all, v_all, c[:, past:], c, past + jnp.arange(t), jnp.arange(past + t))
    return o.reshape(bsz, t, D_ATTN)


def _tail(x, y_conv, o_attn, g_conv, g_attn, w_attn_out, w_o, ln1_g, ln1_b, w_up, w_down, ln2_g, ln2_b):
    y_attn = jnp.einsum('bte,ed->btd', o_attn, w_attn_out)
    mix = jnp.einsum('btd,de->bte', g_conv * y_conv + g_attn * y_attn, w_o)
    x1 = _layer_norm(DN_ALPHA * x + mix, ln1_g, ln1_b)
    h = jnp.square(jax.nn.relu(jnp.einsum('btd,df->btf', x1, w_up)))
    return _layer_norm(DN_ALPHA * x1 + jnp.einsum('btf,fd->btd', h, w_down), ln2_g, ln2_b)


def setup_inputs(seed: int = 0) -> dict:
    key = jax.random.key(seed)
    ks = jax.random.split(key, 24)
    f32 = jnp.float32

    def nrm(k, shape, scale):
        return jax.random.normal(k, shape, f32) * scale

    return {
        "x_prompt": nrm(ks[0], (BATCH, SEQ, D_MODEL), 1.0),
        "x_sample": nrm(ks[1], (DEC_BATCH, DEC_SEQ, D_MODEL), 1.0),
        "cache_conv": nrm(ks[2], (DEPTH, DEC_BATCH, CONV_WIDTH - 1, C_CONV), 0.5),
        "cache_k": nrm(ks[3], (DEPTH, DEC_BATCH, PAST_LEN, N_HEADS, HEAD_DIM), 1.0),
        "cache_v": nrm(ks[4], (DEPTH, DEC_BATCH, PAST_LEN, N_HEADS, HEAD_DIM), 1.0),
        "cache_logf": jax.nn.log_sigmoid(3.0 + nrm(ks[5], (DEPTH, DEC_BATCH, PAST_LEN, N_HEADS), 1.0)),
        "w_in": nrm(ks[6], (DEPTH, D_MODEL, D_IN), D_MODEL ** -0.5),
        "b_f": 3.0 + nrm(ks[7], (DEPTH, N_HEADS), 0.5),
        "w_dw": nrm(ks[8], (DEPTH, CONV_WIDTH, C_CONV), CONV_WIDTH ** -0.5),
        "b_dw": nrm(ks[9], (DEPTH, C_CONV), 0.02),
        "ln_conv_g": 1.0 + nrm(ks[10], (DEPTH, C_CONV), 0.02),
        "ln_conv_b": nrm(ks[11], (DEPTH, C_CONV), 0.02),
        "w_conv_out": nrm(ks[12], (DEPTH, C_CONV, D_MODEL), C_CONV ** -0.5),
        "w_attn_out": nrm(ks[13], (DEPTH, D_ATTN, D_MODEL), D_ATTN ** -0.5),
        "w_o": nrm(ks[14], (DEPTH, D_MODEL, D_MODEL), DN_BETA * D_MODEL ** -0.5),
        "ln1_g": 1.0 + nrm(ks[15], (DEPTH, D_MODEL), 0.02),
        "ln1_b": nrm(ks[16], (DEPTH, D_MODEL), 0.02),
        "w_up": nrm(ks[17], (DEPTH, D_MODEL, D_FF), D_MODEL ** -0.5),
        "w_down": nrm(ks[18], (DEPTH, D_FF, D_MODEL), DN_BETA * D_FF ** -0.5),
        "ln2_g": 1.0 + nrm(ks[19], (DEPTH, D_MODEL), 0.02),
        "ln2_b": nrm(ks[20], (DEPTH, D_MODEL), 0.02),
    }


def reference(x_prompt, x_sample, cache_conv, cache_k, cache_v, cache_logf,
              w_in, b_f, w_dw, b_dw, ln_conv_g, ln_conv_b, w_conv_out, w_attn_out, w_o,
              ln1_g, ln1_b, w_up, w_down, ln2_g, ln2_b):
    xp, xs = x_prompt, x_sample
    conv_p, k_p, v_p, lf_p = [], [], [], []
    conv_s, k_s, v_s, lf_s = [], [], [], []
    hist = CONV_WIDTH - 1
    for l in range(DEPTH):
        u, q, k, v, logf, gc, ga = _in_proj(xp, w_in[l], b_f[l])
        u_ext = jnp.pad(u, ((0, 0), (hist, 0), (0, 0)))
        y_conv = _conv_branch(u_ext, w_dw[l], b_dw[l], ln_conv_g[l], ln_conv_b[l], w_conv_out[l])
        o_attn = _fox_prompt(q, k, v, logf)
        xp = _tail(xp, y_conv, o_attn, gc, ga, w_attn_out[l], w_o[l],
                   ln1_g[l], ln1_b[l], w_up[l], w_down[l], ln2_g[l], ln2_b[l])
        conv_p.append(u_ext[:, -hist:])
        k_p.append(k)
        v_p.append(v)
        lf_p.append(logf)
        u, q, k, v, logf, gc, ga = _in_proj(xs, w_in[l], b_f[l])
        u_ext = jnp.concatenate([cache_conv[l].astype(u.dtype), u], axis=1)
        y_conv = _conv_branch(u_ext, w_dw[l], b_dw[l], ln_conv_g[l], ln_conv_b[l], w_conv_out[l])
        o_attn = _fox_sample(q, k, v, logf, cache_k[l], cache_v[l], cache_logf[l])
        xs = _tail(xs, y_conv, o_attn, gc, ga, w_attn_out[l], w_o[l],
                   ln1_g[l], ln1_b[l], w_up[l], w_down[l], ln2_g[l], ln2_b[l])
        conv_s.append(u_ext[:, -hist:])
        k_s.append(k)
        v_s.append(v)
        lf_s.append(logf)
    return (xp, xs,
            jnp.stack(conv_p), jnp.stack(k_p), jnp.stack(v_p), jnp.stack(lf_p),
            jnp.stack(conv_s), jnp.stack(k_s), jnp.stack(v_s), jnp.stack(lf_s))
```

```python
import numpy as np
from contextlib import ExitStack
import concourse.bass as bass
import concourse.mybir as mybir
from concourse.bass_utils import run_bass_kernel_spmd

F32 = mybir.dt.float32
BF16 = mybir.dt.bfloat16
ALU = mybir.AluOpType
AF = mybir.ActivationFunctionType
ENGS = ("pe", "act", "dve", "pool", "sp")

NCORES = 8
D = 1024
SEQ = 2048
NSEQ = 4
CH = 512
PAST = 1024
DEC = 32
HIST = 30
DIN = 4616
OFF_B, OFF_Q, OFF_K, OFF_V, OFF_F, OFF_GC, OFF_GA = 512, 1024, 1536, 2048, 2560, 2568, 3592
ALPHA = float(2.0 ** 0.25)
EPS = 1e-5
NRING = 8
NEG = -30000.0
SPACE = 5


class Buf:
    __slots__ = ("name", "lw", "rd", "excl")

    def __init__(self, name, excl=False):
        self.name = name
        self.lw = None
        self.rd = []
        self.excl = excl


class Sched:
    def __init__(self, nc, es):
        self.nc = nc
        self.es = es
        self.ops = {e: [] for e in ENGS}
        self.sems = []
        self.cnt = []
        self.esem = {}
        for e in ENGS:
            self.esem[e] = self.new_sem("e_" + e)
        self.seen = {e: {} for e in ENGS}

    def new_sem(self, name):
        h = self.es.enter_context(self.nc.semaphore(name))
        self.sems.append(h)
        self.cnt.append(0)
        return len(self.sems) - 1

    def _waits(self, eng, toks):
        best = {}
        for (s, v) in toks:
            if v > best.get(s, 0):
                best[s] = v
        out = []
        seen = self.seen[eng]
        for s, v in best.items():
            if seen.get(s, 0) >= v:
                continue
            seen[s] = v
            out.append((s, v))
        return out

    def group(self, eng, fns, reads=(), writes=(), acc=(), sem=None, amount=1):
        toks = set()
        for b in reads:
            if b.lw is not None:
                toks.add(b.lw)
            if b.excl:
                toks.update(b.rd)
        for b in writes:
            if b.lw is not None:
                toks.add(b.lw)
            toks.update(b.rd)
        waits = self._waits(eng, toks)
        s = self.esem[eng] if sem is None else sem
        self.cnt[s] += amount
        tok = (s, self.cnt[s])
        n = len(fns)
        for i, fn in enumerate(fns):
            self.ops[eng].append((waits if i == 0 else (), fn, (s, amount) if i == n - 1 else None))
        for b in writes:
            b.lw = tok
            b.rd = []
        for b in acc:
            b.lw = tok
            b.rd = []
        for b in reads:
            if b not in writes:
                b.rd.append(tok)
        return tok

    def op(self, eng, fn, reads=(), writes=()):
        return self.group(eng, [fn], reads, writes)

    def dma(self, eng, fn, sem, reads=(), writes=()):
        return self.group(eng, [fn], reads, writes, sem=sem, amount=16)

    def final_wait(self, eng, toks):
        waits = self._waits(eng, toks)
        self.ops[eng].append((waits, None, None))

    def emit(self):
        nc = self.nc
        sems = self.sems

        def run(eng_name, e):
            for waits, fn, inc in self.ops[eng_name]:
                for (s, v) in waits:
                    e.wait_ge(sems[s], v)
                if fn is None:
                    continue
                ins = fn(e)
                if inc is not None:
                    ins.then_inc(sems[inc[0]], inc[1])

        with nc.Block() as block:
            @block.tensor
            def _(e):
                run("pe", e)

            @block.scalar
            def _(e):
                run("act", e)

            @block.vector
            def _(e):
                run("dve", e)

            @block.gpsimd
            def _(e):
                run("pool", e)

            @block.sync
            def _(e):
                run("sp", e)


def _units():
    U = {}
    lst = []

    def add(key, spec):
        U[key] = len(lst)
        lst.append(spec)

    for i in range(2):
        add(("a", i), ("w_in", 0, 1024, 256 * i, 256))
    for i in range(2):
        add(("b", i), ("w_in", 0, 1024, OFF_B + 256 * i, 256))
    for i in range(2):
        add(("q", i), ("w_in", 0, 1024, OFF_Q + 256 * i, 256))
    for hh in range(2):
        add(("k", hh), ("w_in", 512 * hh, 512, OFF_K, 512))
    for hh in range(2):
        add(("v", hh), ("w_in", 512 * hh, 512, OFF_V, 512))
    for i in range(4):
        add(("gc", i), ("w_in", 0, 1024, OFF_GC + 256 * i, 256))
    for i in range(4):
        add(("ga", i), ("w_in", 0, 1024, OFF_GA + 256 * i, 256))
    for eh in range(2):
        add(("co", eh), ("w_co", 0, 512, 512 * eh, 512))
    for eh in range(2):
        add(("ao", eh), ("w_ao", 0, 512, 512 * eh, 512))
    for eh in range(2):
        for hh in range(2):
            add(("o", eh, hh), ("w_o", 512 * hh, 512, 512 * eh, 512))
    for fp in range(16):
        add(("up", fp), ("w_up", 0, 1024, 256 * fp, 256))
    for eh in range(2):
        for g in range(8):
            add(("dn", eh, g), ("w_down", 512 * g, 512, 512 * eh, 512))
    return U, lst


_PHASES = None


class _Stop(Exception):
    pass


def build_program(dbg=None, nseq=NSEQ):
    nc = bass.Bass("TRN2", target_bir_lowering=False)

    _marks = []

    def mark(name):
        if _PHASES is not None:
            _marks.append((name, {e: sum(1 for o in S_ref[0].ops[e] if o[1] is not None) for e in ENGS}))
        if dbg is not None and name == dbg:
            raise _Stop()

    S_ref = [None]

    def din(name, shape):
        return nc.dram_tensor(name, list(shape), F32, kind="ExternalInput").ap()

    def dout(name, shape):
        return nc.dram_tensor(name, list(shape), F32, kind="ExternalOutput").ap()

    xp = din("xp", [nseq * SEQ, D])
    xs = din("xs", [DEC, D])
    cconv = din("cconv", [HIST, 512])
    ck = din("ck", [PAST, 512])
    cv = din("cv", [PAST, 512])
    clf = din("clf", [PAST, 8])
    W = {
        "w_in": din("w_in", [D, DIN]),
        "w_co": din("w_co", [512, D]),
        "w_ao": din("w_ao", [512, D]),
        "w_o": din("w_o", [D, D]),
        "w_up": din("w_up", [D, 4096]),
        "w_down": din("w_down", [4096, D]),
    }
    b_f = din("b_f", [1, 8])
    wdw_d = din("wdw", [128, 4, 31])
    cpar_d = din("cpar", [128, 4, 3])
    lnp_d = din("lnp", [4, D])

    y_p = dout("y_p", [nseq * SEQ, D])
    y_s = dout("y_s", [DEC, D])
    conv_p = dout("conv_p", [nseq, HIST, 512])
    k_p = dout("k_p", [nseq * SEQ, 512])
    v_p = dout("v_p", [nseq * SEQ, 512])
    lf_p = dout("lf_p", [nseq * SEQ, 8])
    conv_s = dout("conv_s", [HIST, 512])
    k_s = dout("k_s", [DEC, 512])
    v_s = dout("v_s", [DEC, 512])
    lf_s = dout("lf_s", [DEC, 8])
    dbg_o = dout("dbg_o", [128, 35 * 8]) if dbg == "DUMP" else None

    UIDX, ULIST = _units()
    NU = len(ULIST)
    wsc = nc.dram_tensor("wsc", [NU, 128, 2048], BF16, kind="Internal").ap()

    with ExitStack() as es:
        S = Sched(nc, es)
        S_ref[0] = S
        es.enter_context(nc.allow_non_contiguous_dma(reason="tiny transposing state DMAs"))

        def sb(name, shape, dt):
            return es.enter_context(nc.sbuf_tensor("s_" + name, list(shape), dt))

        ring = [sb(f"ring{i}", [128, 2048], BF16) for i in range(NRING)]
        ring_b = [Buf(f"ring{i}") for i in range(NRING)]
        ring_s = [S.new_sem(f"ring{i}") for i in range(NRING)]
        Bt = sb("Bt", [128, 4, D], F32)
        B_b = [Buf(f"B{i}") for i in range(4)]
        B_s = [S.new_sem(f"Bld{i}") for i in range(4)]
        Bst_s = [S.new_sem(f"Bst{i}") for i in range(4)]
        Xb = sb("Xb", [128, 4, D], BF16)
        Xb_b = [Buf(f"Xb{i}") for i in range(4)]
        Xb_s = S.new_sem("Xb")
        xT = sb("xT", [128, 8, CH], BF16)
        xT_b = Buf("xT")
        ubuf = sb("ubuf", [128, 4, HIST + CH], F32)
        u_b = [Buf(f"u{i}") for i in range(4)]
        cacc = sb("cacc", [128, 4, CH], F32)
        cacc_b = [Buf(f"cacc{i}") for i in range(4)]
        ybf = sb("ybf", [128, 4, CH], BF16)
        ybf_b = [Buf(f"ybf{i}") for i in range(4)]
        zT = sb("zT", [128, 4, CH], BF16)
        zT_b = [Buf(f"zT{i}") for i in range(4)]
        Qpad = sb("Qpad", [128, 8, CH], BF16)
        Q_b = [Buf(f"Q{i}") for i in range(8)]
        KT = sb("KT", [128, 4, SEQ], BF16)
        KT_b = [Buf(f"KT{i}") for i in range(16)]
        Vaug = sb("Vaug", [128, 16, 4, 192], BF16)
        V_b = [Buf(f"V{i}") for i in range(16)]
        negc = sb("negc", [128, 17, 8], F32)
        negc_b = Buf("negc")
        carry = sb("carry", [128, 18, 8], F32)
        carry_b = Buf("carry")
        NPT = 4
        PT = [sb(f"PT{i}", [128, CH], BF16) for i in range(NPT)]
        PT_b = [Buf(f"PT{i}") for i in range(NPT)]
        OT = sb("OT", [128, 4, CH], BF16)
        OT_b = [Buf(f"OT{i}") for i in range(4)]
        den = [sb(f"den{i}", [128, CH], F32) for i in range(1)]
        den_b = [Buf(f"den{i}") for i in range(1)]
        NA = 3
        At = [sb(f"At{i}", [128, CH], F32) for i in range(NA)]
        At_b = [Buf(f"At{i}") for i in range(NA)]
        mT = sb("mT", [128, 8, CH], BF16)
        MX = [[Buf(f"mT{i}_{t}") for t in range(4)] for i in range(8)]
        cvst = sb("cvst", [128, 4, HIST], F32)
        cvst_b = [Buf(f"cvst{i}") for i in range(4)]
        rstd_t = sb("rstd_t", [128, CH], F32)
        rstd_b = Buf("rstd")
        hT = sb("hT", [128, 16, CH], BF16)
        hT_b = [Buf(f"hT{i}") for i in range(16)]
        NKV = 2
        kvst = [sb(f"kvst{i}", [128, 512], F32) for i in range(NKV)]
        kvst_b = [Buf(f"kvst{i}") for i in range(NKV)]
        kvst_s = [S.new_sem(f"kvst{i}") for i in range(NKV)]
        kb = [sb(f"kb{i}", [128, 512], BF16) for i in range(2)]
        kb_b = [Buf(f"kb{i}") for i in range(2)]
        zt = [sb(f"zt{i}", [128, 3, 4, 8], F32) for i in range(1)]
        zt_b = [Buf(f"zt{i}") for i in range(1)]
        zt_s = [S.new_sem(f"zt{i}") for i in range(1)]
        lnst = [sb(f"lnst{i}", [128, 2, 6], F32) for i in range(4)]
        lnmv = [sb(f"lnmv{i}", [128, 4], F32) for i in range(4)]
        ln_b = [Buf(f"ln{i}") for i in range(4)]
        lnst4, lnmv4, ln4_b = lnst, lnmv, ln_b
        ident = sb("ident", [128, 128], BF16)
        maskt = sb("maskt", [128, 128], BF16)
        Uf = sb("Uf", [128, 128], F32)
        onesf = sb("onesf", [128, 128], F32)
        onesM = sb("onesM", [128, 128], BF16)
        negones = sb("negones", [128, 128], BF16)
        Dq = [sb(f"Dq{i}", [128, 4, 128], BF16) for i in range(2)]
        Dq_b = [Buf(f"Dq{i}") for i in range(2)]
        cst = sb("cst", [128, 4], F32)
        bfb = sb("bfb", [128, 8], F32)
        Wf = sb("Wf", [128, 8, 8], BF16)
        wdw = sb("wdw", [128, 4, 31], F32)
        cpar = sb("cpar", [128, 4, 3], F32)
        lnp = sb("lnp", [128, 4, D], F32)
        const_b = Buf("const")
        cd_b = Buf("cdma")
        const_s = S.new_sem("const")
        wf_s = S.new_sem("wfc")
        hist_s = S.new_sem("hist")
        ck_s = S.new_sem("ck")
        ck_s2 = S.new_sem("ck2")
        cv_s = S.new_sem("cv")
        cvo_s = [S.new_sem(f"cvo{i}") for i in range(4)]
        wsc_b = Buf("wsc")
        wsc_s = S.new_sem("wsc")
        out_s = S.new_sem("outs")
        out_cnt = [0]

        banks = [es.enter_context(nc.psum_tensor(f"bank{i}", [128, 512], F32)) for i in range(8)]
        bank_b = [Buf(f"bank{i}", excl=True) for i in range(8)]
        pstate = {"i": 0, "held": set()}

        def pget(hold=False):
            for _ in range(16):
                i = pstate["i"]
                pstate["i"] = (i + 1) % 8
                if i in pstate["held"]:
                    continue
                if hold:
                    pstate["held"].add(i)
                return i
            raise RuntimeError("no psum bank")

        def prel(i):
            pstate["held"].discard(i)

        def MM(out, lhsT, rhs, start, stop):
            return lambda e: e.matmul(out, lhsT=lhsT, rhs=rhs, start=start, stop=stop)

        def TR(out, in_):
            return lambda e: e.transpose(out=out, in_=in_, identity=ident[:])

        def ACT(out, in_, func, **kw):
            return lambda e: e.activation(out=out, in_=in_, func=func, **kw)

        def TT(out, in0, in1, op):
            return lambda e: e.tensor_tensor(out=out, in0=in0, in1=in1, op=op)

        def TS(out, in0, s1, s2, op0, op1=None):
            if op1 is None:
                return lambda e: e.tensor_scalar(out=out, in0=in0, scalar1=s1, scalar2=None, op0=op0)
            return lambda e: e.tensor_scalar(out=out, in0=in0, scalar1=s1, scalar2=s2, op0=op0, op1=op1)

        def STT(out, in0, scalar, in1, op0, op1):
            return lambda e: e.scalar_tensor_tensor(out=out, in0=in0, scalar=scalar, in1=in1, op0=op0, op1=op1)

        def CP(out, in_):
            return lambda e: e.tensor_copy(out=out, in_=in_)

        def DMA(out, in_):
            return lambda e: e.dma_start(out=out, in_=in_)

        def store(out, in_, reads, sem):
            return S.dma("pool", DMA(out, in_), sem, reads=reads)

        def cdma(eng, out, in_):
            S.group(eng, [DMA(out, in_)], sem=const_s, amount=16)

        cdma("sp", bfb[:], b_f.partition_broadcast(128).rearrange("p a h -> p (a h)"))
        cdma("sp", wdw[:], wdw_d)
        cdma("sp", cpar[:], cpar_d)
        for i in range(4):
            cdma("sp", lnp[:, i, :], lnp_d[i:i + 1, :].partition_broadcast(128).rearrange("p a d -> p (a d)"))
        cd_b.lw = (const_s, S.cnt[const_s])
        wf_b = Buf("wf")
        S.dma("pool", DMA(Wf[:], W["w_in"][:, OFF_F:OFF_F + 8].rearrange("(kc p) e -> p kc e", p=128)), wf_s, writes=[wf_b])

        cb = [const_b]
        S.op("pool", lambda e: e.memset(ident[:], 0.0), writes=cb)
        S.op("pool", lambda e: e.affine_select(out=ident[:], in_=ident[:], pattern=[[-1, 128]], compare_op=ALU.not_equal,
                                               fill=1.0, base=0, channel_multiplier=1), writes=cb)
        S.op("pool", lambda e: e.memset(maskt[:], 0.0), writes=cb)
        S.op("pool", lambda e: e.affine_select(out=maskt[:], in_=maskt[:], pattern=[[1, 128]], compare_op=ALU.is_ge,
                                               fill=NEG, base=0, channel_multiplier=-1), writes=cb)
        S.op("pool", lambda e: e.memset(Uf[:], 1.0), writes=cb)
        S.op("pool", lambda e: e.affine_select(out=Uf[:], in_=Uf[:], pattern=[[1, 128]], compare_op=ALU.is_ge,
                                               fill=0.0, base=0, channel_multiplier=-1), writes=cb)
        S.op("pool", lambda e: e.memset(onesf[:], 1.0), writes=cb)
        S.op("pool", lambda e: e.memset(onesM[:], 1.0 / 512.0), writes=cb)
        S.op("pool", lambda e: e.memset(negones[:], -1.0), writes=cb)
        S.op("pool", lambda e: e.memset(cst[:, 0:1], 1.0), writes=cb)
        S.op("pool", lambda e: e.memset(cst[:, 1:2], EPS), writes=cb)
        S.op("pool", lambda e: e.memset(cst[:, 2:4], 0.0), writes=cb)
        S.op("pool", lambda e: e.memset(Qpad[:], 0.0), writes=Q_b)
        S.op("pool", lambda e: e.memset(Vaug[:, :, :, 64:128], 1.0), writes=V_b)
        S.op("pool", lambda e: e.memset(Xb[:, 0, :], 0.0), writes=[Xb_b[0]])
        S.op("pool", lambda e: e.memset(Bt[:, 0, :], 0.0), writes=[B_b[0]])
        S.op("pool", lambda e: e.memset(zt[0][:], 0.0), writes=[zt_b[0]])

        rpos = [0]

        def wload(key):
            slot = rpos[0] % NRING
            rpos[0] += 1
            S.dma("sp", DMA(ring[slot][:], wsc[UIDX[key]]), ring_s[slot], reads=[wsc_gb[ugrp[UIDX[key]]]], writes=[ring_b[slot]])
            return slot

        def rF(slot):
            return ring[slot][:].rearrange("p (k e) -> p k e", k=8)

        def rT(slot):
            return ring[slot][:].rearrange("p (k e) -> p k e", k=4)

        rr = {"A": 0, "kv": 0, "kb": 0, "pt": 0, "zt": 0, "ln": 0, "den": 0, "x1b": 0}

        def nxt(name, n):
            i = rr[name]
            rr[name] = (i + 1) % n
            return i

        def cumsum_tiles(kt_s, n, lf_fn, lf_bufs):
            bk = pget()
            fns = []
            for i in range(n):
                fns.append(MM(banks[bk][:, i * 8:(i + 1) * 8], Uf[:], lf_fn(i), True, i == 0))
                for j in range(i):
                    fns.append(MM(banks[bk][:, i * 8:(i + 1) * 8], onesf[:], lf_fn(j), False, j == i - 1))
                for j in range(i + 1):
                    fns.append(MM(banks[bk][:, (n + i) * 8:(n + i + 1) * 8], onesf[:], lf_fn(j), j == 0, j == i))
            S.group("pe", fns, reads=list(lf_bufs) + [const_b], writes=[bank_b[bk]])
            cbc = carry[:, kt_s:kt_s + 1, :].to_broadcast([128, n, 8])
            S.op("dve", STT(negc[:, kt_s:kt_s + n, :], banks[bk][:, 0:n * 8].rearrange("p (t h) -> p t h", h=8), -1.0, cbc, ALU.mult, ALU.subtract),
                 reads=[bank_b[bk], carry_b], writes=[negc_b])
            S.op("dve", TT(carry[:, kt_s + 1:kt_s + n + 1, :], banks[bk][:, n * 8:2 * n * 8].rearrange("p (t h) -> p t h", h=8), cbc, ALU.add),
                 reads=[bank_b[bk], carry_b], writes=[carry_b])

        def prefetch_x(ck):
            if ck is None:
                return
            if ck["nt"] == 1:
                S.dma("pool", DMA(Xb[0:ck["nreal"], 0, :], ck["xsrc"]), Xb_s, writes=[Xb_b[0]])
            else:
                S.dma("pool", DMA(Xb[:, :, :], ck["xsrc"].rearrange("(t p) d -> p t d", p=128)), Xb_s, writes=Xb_b)

        def ln_stats_norm(tt):
            li = nxt("ln", 2)
            st, mv, lb = lnst[li], lnmv[li], ln_b[li]
            S.op("dve", lambda e: e.bn_stats(out=st[:, 0, :], in_=Bt[:, tt, 0:512]), reads=[B_b[tt]], writes=[lb])
            S.op("dve", lambda e: e.bn_stats(out=st[:, 1, :], in_=Bt[:, tt, 512:1024]), reads=[B_b[tt], lb], writes=[lb])
            S.op("dve", lambda e: e.bn_aggr(out=mv[:, 0:2], in_=st[:]), reads=[lb], writes=[lb])
            S.op("act", ACT(mv[:, 2:3], mv[:, 1:2], AF.Ln, bias=cst[:, 1:2], scale=1.0), reads=[lb, const_b], writes=[lb])
            S.op("act", ACT(mv[:, 3:4], mv[:, 2:3], AF.Exp, scale=-0.5), reads=[lb], writes=[lb])
            S.op("dve", TS(Bt[:, tt, :], Bt[:, tt, :], mv[:, 0:1], mv[:, 3:4], ALU.subtract, ALU.mult), reads=[lb, B_b[tt]], writes=[B_b[tt]])

        def ln_stats_norm_staged(tt):
            li = nxt("ln", 2)
            st, mv, lb = lnst[li], lnmv[li], ln_b[li]
            S.op("dve", lambda e: e.bn_stats(out=st[:, 0, :], in_=Bt[:, tt, 0:512]), reads=[B_b[tt]], writes=[lb])
            S.op("dve", lambda e: e.bn_stats(out=st[:, 1, :], in_=Bt[:, tt, 512:1024]), reads=[B_b[tt], lb], writes=[lb])
            S.op("dve", lambda e: e.bn_aggr(out=mv[:, 0:2], in_=st[:]), reads=[lb], writes=[lb])
            for _ in range(SPACE):
                yield
            S.op("act", ACT(mv[:, 2:3], mv[:, 1:2], AF.Ln, bias=cst[:, 1:2], scale=1.0), reads=[lb, const_b], writes=[lb])
            S.op("act", ACT(mv[:, 3:4], mv[:, 2:3], AF.Exp, scale=-0.5), reads=[lb], writes=[lb])
            for _ in range(SPACE):
                yield
            S.op("dve", TS(Bt[:, tt, :], Bt[:, tt, :], mv[:, 0:1], mv[:, 3:4], ALU.subtract, ALU.mult), reads=[lb, B_b[tt]], writes=[B_b[tt]])

        def ln_affine(tt, gi):
            S.op("dve", TT(Bt[:, tt, :], Bt[:, tt, :], lnp[:, gi, :], ALU.mult), reads=[B_b[tt], cd_b], writes=[B_b[tt]])
            S.op("dve", TT(Bt[:, tt, :], Bt[:, tt, :], lnp[:, gi + 1, :], ALU.add), reads=[B_b[tt], cd_b], writes=[B_b[tt]])


        def fm_group(N, bk, slot, el, kn, rhs_fn, rbufs):
            if kn == 8:
                wv = rF(slot)
                fns = [MM(banks[bk][:, 0:N], wv[:, kc, el * 128:(el + 1) * 128], rhs_fn(kc), kc == 0, kc == 7) for kc in range(8)]
            else:
                wv = rT(slot)
                fns = [MM(banks[bk][:, 0:N], wv[:, kc, el * 128:(el + 1) * 128], rhs_fn(kc), kc == 0, kc == 3) for kc in range(4)]
            S.group("pe", fns, reads=[ring_b[slot]] + rbufs, writes=[bank_b[bk]])

        pending = []
        ln2_task = [None]

        def ln2_step(n):
            for _ in range(n):
                if ln2_task[0] is None:
                    return
                try:
                    next(ln2_task[0])
                except StopIteration:
                    ln2_task[0] = None
                    return

        def flush_stores():
            while pending:
                o_, i_, r_, s_ = pending.pop(0)
                store(o_, i_, r_, s_)

        def bg(ck, n):
            if ck is None or ck.get("cg") is None:
                return
            for _ in range(n):
                try:
                    next(ck["cg"])
                except StopIteration:
                    ck["cg"] = None
                    return

        def front_a(ck):
            if ck["seq"] is not None and ck["seq_first"]:
                for e_ in range(4):
                    S.op("pool", lambda e, e_=e_: e.memset(ubuf[:, e_, 0:HIST], 0.0), writes=[u_b[e_]])
            nt, nreal, conv_out = ck["nt"], ck["nreal"], ck["conv_out"]
            N = nt * 128
            treal = (nt - 1) * 128 + nreal

            for tt in range(nt):
                bk = pget()
                bv = banks[bk][:].bitcast(BF16).rearrange("p (k t) -> p k t", k=8)
                S.group("pe", [TR(bv[:, kc, :], Xb[:, tt, kc * 128:(kc + 1) * 128]) for kc in range(8)],
                        reads=[Xb_b[tt], const_b], writes=[bank_b[bk]])
                S.op("act", ACT(xT[:, :, tt * 128:(tt + 1) * 128], bv, AF.Copy), reads=[bank_b[bk]], writes=[xT_b])
            mark("xT")

        def front_b(ck):
            nt, nreal, conv_out = ck["nt"], ck["nreal"], ck["conv_out"]
            N = nt * 128
            treal = (nt - 1) * 128 + nreal
            x_rhs = lambda kc: xT[:, kc, 0:N]

            for i in range(2):
                sa = wload(("a", i))
                abk = []
                for el in range(2):
                    bk = pget(hold=True)
                    fm_group(N, bk, sa, el, 8, x_rhs, [xT_b])
                    abk.append(bk)
                sbq = wload(("b", i))
                for el in range(2):
                    e_ = 2 * i + el
                    bk = pget()
                    fm_group(N, bk, sbq, el, 8, x_rhs, [xT_b])
                    ai = nxt("A", NA)
                    S.op("act", ACT(At[ai][:, 0:N], banks[bk][:, 0:N], AF.Sigmoid), reads=[bank_b[bk]], writes=[At_b[ai]])
                    S.op("dve", TT(ubuf[:, e_, HIST:HIST + N], banks[abk[el]][:, 0:N], At[ai][:, 0:N], ALU.mult),
                         reads=[bank_b[abk[el]], At_b[ai]], writes=[u_b[e_]])
                    prel(abk[el])
            mark("glu")

            def conv_gen():
                for e_ in range(4):
                    S.op("dve", TS(cacc[:, e_, 0:N], ubuf[:, e_, 0:N], wdw[:, e_, 0:1], cpar[:, e_, 0:1], ALU.mult, ALU.add),
                         reads=[u_b[e_], cd_b], writes=[cacc_b[e_]])
                    yield
                    for j in range(1, 31):
                        S.op("dve", STT(cacc[:, e_, 0:N], ubuf[:, e_, j:j + N], wdw[:, e_, j:j + 1], cacc[:, e_, 0:N], ALU.mult, ALU.add),
                             reads=[u_b[e_]], writes=[cacc_b[e_]])
                        yield
                if conv_out is not None:
                    for e_ in range(4):
                        S.op("dve", CP(cvst[:, e_, :], ubuf[:, e_, treal:treal + HIST]), reads=[u_b[e_]], writes=[cvst_b[e_]])
                        store(conv_out[:, e_ * 128:(e_ + 1) * 128].rearrange("t p -> p t"), cvst[:, e_, :], [cvst_b[e_]], cvo_s[e_])
                for e_ in range(4):
                    S.op("pool", CP(ubuf[:, e_, 0:HIST], ubuf[:, e_, N:N + HIST]), reads=[], writes=[u_b[e_]])
                yield
                for e_ in range(4):
                    S.op("pool", CP(ybf[:, e_, 0:N], cacc[:, e_, 0:N]), reads=[cacc_b[e_]], writes=[ybf_b[e_]])
                for _ in range(SPACE):
                    yield
                bm = pget(hold=True)
                S.group("pe", [MM(banks[bm][:, 0:N], onesM[:], ybf[:, e_, 0:N], e_ == 0, e_ == 3) for e_ in range(4)],
                        reads=ybf_b + [const_b], writes=[bank_b[bm]])
                for _ in range(SPACE):
                    yield
                for e_ in range(4):
                    S.op("dve", TT(cacc[:, e_, 0:N], cacc[:, e_, 0:N], banks[bm][:, 0:N], ALU.subtract), reads=[bank_b[bm], cacc_b[e_]], writes=[cacc_b[e_]])
                prel(bm)
                for _ in range(SPACE):
                    yield
                for e_ in range(4):
                    S.op("pool", TT(ybf[:, e_, 0:N], cacc[:, e_, 0:N], cacc[:, e_, 0:N], ALU.mult), reads=[cacc_b[e_]], writes=[ybf_b[e_]])
                for _ in range(SPACE):
                    yield
                bv_ = pget(hold=True)
                S.group("pe", [MM(banks[bv_][:, 0:N], onesM[:], ybf[:, e_, 0:N], e_ == 0, e_ == 3) for e_ in range(4)],
                        reads=ybf_b + [const_b], writes=[bank_b[bv_]])
                for _ in range(SPACE):
                    yield
                S.op("act", ACT(rstd_t[:, 0:N], banks[bv_][:, 0:N], AF.Ln, bias=cst[:, 1:2], scale=1.0), reads=[bank_b[bv_], const_b], writes=[rstd_b])
                S.op("act", ACT(rstd_t[:, 0:N], rstd_t[:, 0:N], AF.Exp, scale=-0.5), reads=[rstd_b], writes=[rstd_b])
                prel(bv_)
                for _ in range(SPACE):
                    yield
                for e_ in range(4):
                    S.op("dve", TT(cacc[:, e_, 0:N], cacc[:, e_, 0:N], rstd_t[:, 0:N], ALU.mult), reads=[cacc_b[e_], rstd_b], writes=[cacc_b[e_]])
                for _ in range(SPACE):
                    yield
                for e_ in range(4):
                    S.op("act", ACT(zT[:, e_, 0:N], cacc[:, e_, 0:N], AF.Silu, bias=cpar[:, e_, 2:3], scale=cpar[:, e_, 1:2]),
                         reads=[cacc_b[e_], cd_b], writes=[zT_b[e_]])
                yield

            ck["cg"] = conv_gen()

        def mid_qkvf(ck):
            nt, nreal, kt0, seq_first = ck["nt"], ck["nreal"], ck["kt0"], ck["seq_first"]
            xrows, k_o, v_o, lf_all = ck["xrows"], ck["k_o"], ck["v_o"], ck["lf_all"]
            N = nt * 128
            x_rhs = lambda kc: xT[:, kc, 0:N]

            for i in range(2):
                sq = wload(("q", i))
                for el in range(2):
                    pr = 2 * i + el
                    bk = pget()
                    fm_group(N, bk, sq, el, 8, x_rhs, [xT_b])
                    S.op("act", ACT(Qpad[0:64, 2 * pr, 0:N], banks[bk][0:64, 0:N], AF.Copy, scale=0.125),
                         reads=[bank_b[bk]], writes=[Q_b[2 * pr]])
                    S.op("act", ACT(Qpad[64:128, 2 * pr + 1, 0:N], banks[bk][64:128, 0:N], AF.Copy, scale=0.125),
                         reads=[bank_b[bk]], writes=[Q_b[2 * pr + 1]])
            mark("q")

            sk = [wload(("k", 0)), wload(("k", 1))]
            kbi = {}
            for tt in range(nt + 1):
                if tt < nt:
                    bk = pget()
                    S.group("pe", [MM(banks[bk][:, 0:512], xT[:, kc, tt * 128:(tt + 1) * 128], rT(sk[kc // 4])[:, kc % 4, :], kc == 0, kc == 7)
                                   for kc in range(8)], reads=[xT_b, ring_b[sk[0]], ring_b[sk[1]]], writes=[bank_b[bk]])
                    si = nxt("kv", NKV)
                    S.op("act", ACT(kvst[si][:], banks[bk][:], AF.Copy), reads=[bank_b[bk]], writes=[kvst_b[si]])
                    ki = nxt("kb", 2)
                    kbi[tt] = ki
                    S.op("act", ACT(kb[ki][:], banks[bk][:], AF.Copy), reads=[bank_b[bk]], writes=[kb_b[ki]])
                    store(k_o(tt), kvst[si][0:nreal, :], [kvst_b[si]], kvst_s[si])
                if tt >= 1:
                    t2 = tt - 1
                    kt = kt0 + t2
                    ki = kbi[t2]
                    bk2 = pget()
                    bv = banks[bk2][:].bitcast(BF16)[:, 0:512].rearrange("p (k t) -> p k t", k=4)
                    S.group("pe", [TR(bv[:, pr, :], kb[ki][:, pr * 128:(pr + 1) * 128]) for pr in range(4)],
                            reads=[kb_b[ki], const_b], writes=[bank_b[bk2]])
                    S.op("act", ACT(KT[:, :, kt * 128:(kt + 1) * 128], bv, AF.Copy), reads=[bank_b[bk2]], writes=[KT_b[kt]])
            mark("k")

            sv = [wload(("v", 0)), wload(("v", 1))]
            for tt in range(nt):
                kt = kt0 + tt
                bk = pget()
                S.group("pe", [MM(banks[bk][:, 0:512], xT[:, kc, tt * 128:(tt + 1) * 128], rT(sv[kc // 4])[:, kc % 4, :], kc == 0, kc == 7)
                               for kc in range(8)], reads=[xT_b, ring_b[sv[0]], ring_b[sv[1]]], writes=[bank_b[bk]])
                si = nxt("kv", NKV)
                S.op("act", ACT(kvst[si][:], banks[bk][:], AF.Copy), reads=[bank_b[bk]], writes=[kvst_b[si]])
                store(v_o(tt), kvst[si][0:nreal, :], [kvst_b[si]], kvst_s[si])
                vv = banks[bk][:].rearrange("p (r a d) -> p r a d", r=4, a=2)
                S.op("act", ACT(Vaug[:, kt, :, 0:64], vv[:, :, 0, :], AF.Copy), reads=[bank_b[bk]], writes=[V_b[kt]])
                S.op("act", ACT(Vaug[:, kt, :, 128:192], vv[:, :, 1, :], AF.Copy), reads=[bank_b[bk]], writes=[V_b[kt]])
            mark("v")

            if seq_first:
                S.op("dve", lambda e: e.memset(carry[:, kt0, :], 0.0), writes=[carry_b])
            bk = pget()
            fns = []
            for tt in range(nt):
                fns += [MM(banks[bk][:, tt * 8:(tt + 1) * 8], xT[:, kc, tt * 128:(tt + 1) * 128], Wf[:, kc, :], kc == 0, kc == 7) for kc in range(8)]
            S.group("pe", fns, reads=[xT_b, wf_b], writes=[bank_b[bk]])
            zi = 0
            z = zt[zi]
            zb = zt_b[zi]
            zz = lambda r: z[:, r, 0:nt, :]
            S.op("dve", TT(zz(0), banks[bk][:, 0:nt * 8].rearrange("p (t h) -> p t h", h=8), bfb[:].unsqueeze(1).to_broadcast([128, nt, 8]), ALU.add),
                 reads=[bank_b[bk], cd_b], writes=[zb])
            S.op("dve", STT(zz(1), zz(0), -1.0, zz(0), ALU.mult, ALU.min), reads=[zb], writes=[zb])
            S.op("act", ACT(zz(1), zz(1), AF.Exp), reads=[zb], writes=[zb])
            S.op("act", ACT(zz(1), zz(1), AF.Ln, bias=cst[:, 0:1], scale=1.0), reads=[zb, const_b], writes=[zb])
            S.op("dve", STT(zz(2), zz(0), 0.0, zz(1), ALU.min, ALU.subtract), reads=[zb], writes=[zb])
            if nt == 1:
                S.dma("pool", DMA(lf_all, z[0:nreal, 2, 0, :]), zt_s[zi], reads=[zb])
            else:
                S.dma("pool", DMA(lf_all.rearrange("(t p) h -> p t h", p=128), z[:, 2, 0:nt, :]), zt_s[zi], reads=[zb])
            cumsum_tiles(kt0, nt, lambda i: z[:, 2, i, :], [zb])
            mark("f")
            bg(ck, 8)

            mark("bias")

        def mid_rest(ck):
            nt, nreal, kt0, seq_first = ck["nt"], ck["nreal"], ck["kt0"], ck["seq_first"]
            xrows = ck["xrows"]
            N = nt * 128
            x_rhs = lambda kc: xT[:, kc, 0:N]

            nkt = kt0 + nt
            accb = {}

            def begin_head(h):
                dqi = h % 2
                S.op("pool", TT(Dq[dqi][:, 0:nt, :], ident[:].unsqueeze(1).to_broadcast([128, nt, 128]),
                                negc[:, kt0:kt0 + nt, h:h + 1].to_broadcast([128, nt, 128]), ALU.mult),
                     reads=[negc_b, const_b], writes=[Dq_b[dqi]])
                accb[h] = pget(hold=True)

            def rec_S(h, kt):
                pr = h // 2
                dqi = h % 2
                dqv = Dq[dqi][:].rearrange("p j q -> p (j q)")
                c0 = max(0, kt - kt0) * 128
                bk = pget()
                fns = [MM(banks[bk][:, c0:N], KT[:, pr, kt * 128:(kt + 1) * 128], Qpad[:, h, c0:N], True, False),
                       MM(banks[bk][:, c0:N], negones[:], dqv[:, c0:N], False, kt < kt0)]
                if kt >= kt0:
                    fns.append(MM(banks[bk][:, c0:c0 + 128], ident[:], maskt[:], False, True))
                S.group("pe", fns, reads=[KT_b[kt], Q_b[h], const_b, Dq_b[dqi]], writes=[bank_b[bk]])
                return bk, c0

            def rec_E(h, kt, bk, c0):
                pi = nxt("pt", NPT)
                S.op("act", ACT(PT[pi][:, c0:N], banks[bk][:, c0:N], AF.Exp, bias=negc[:, kt, h:h + 1], scale=1.0),
                     reads=[bank_b[bk], negc_b], writes=[PT_b[pi]])
                return pi

            def rec_PV(h, kt, pi, c0):
                pr, odd = h // 2, h % 2
                acc = accb[h]
                vsl = slice(64, 192) if odd else slice(0, 128)
                fn = [MM(banks[acc][:, c0:N], Vaug[:, kt, pr, vsl], PT[pi][:, c0:N], kt == 0, kt == nkt - 1)]
                if kt == 0:
                    S.group("pe", fn, reads=[V_b[kt], PT_b[pi]], writes=[bank_b[acc]])
                else:
                    S.group("pe", fn, reads=[V_b[kt], PT_b[pi]], acc=[bank_b[acc]])

            def normalise(h):
                pr, odd = h // 2, h % 2
                acc = accb[h]
                lo, hi = (64, 128) if odd else (0, 64)
                dlo, dhi = (0, 64) if odd else (64, 128)
                S.op("act", ACT(den[0][lo:hi, 0:N], banks[acc][dlo:dhi, 0:N], AF.Copy), reads=[bank_b[acc]], writes=[den_b[0]])
                S.op("dve", lambda e: e.reciprocal(out=den[0][lo:hi, 0:N], in_=den[0][lo:hi, 0:N]), reads=[den_b[0]], writes=[den_b[0]])
                S.op("dve", TT(OT[lo:hi, pr, 0:N], banks[acc][lo:hi, 0:N], den[0][lo:hi, 0:N], ALU.mult),
                     reads=[bank_b[acc], den_b[0], OT_b[pr]], writes=[OT_b[pr]])
                prel(acc)

            items = [(h, kt) for h in range(8) for kt in range(nkt)]
            bgn = max(1, -(-80 // len(items)))
            ln2n = max(1, -(-56 // len(items)))
            norm_q = []
            DEPTH = 2
            pend = []
            nxt_i = 0
            for i, (h, kt) in enumerate(items):
                while nxt_i < len(items) and nxt_i <= i + DEPTH:
                    h2, kt2 = items[nxt_i]
                    if kt2 == 0:
                        begin_head(h2)
                    pend.append(rec_S(h2, kt2))
                    nxt_i += 1
                cur = pend.pop(0)
                pi = rec_E(h, kt, cur[0], cur[1])
                rec_PV(h, kt, pi, cur[1])
                if kt == nkt - 1:
                    norm_q.append([h, 2])
                for nq in list(norm_q):
                    if nq[1] == 0:
                        normalise(nq[0])
                        norm_q.remove(nq)
                    else:
                        nq[1] -= 1
                bg(ck, bgn)
                ln2_step(ln2n)
            for nq in norm_q:
                normalise(nq[0])
            ln2_step(100)
            flush_stores()
            bg(ck, 100000)
            mark("attn")
            mark("convln")

            sco = sao = None
            for ep in range(4):
                if ep % 2 == 0:
                    sco = wload(("co", ep // 2))
                    sao = wload(("ao", ep // 2))
                sgc = wload(("gc", ep))
                sga = wload(("ga", ep))
                for el in range(2):
                    e_ = 2 * ep + el
                    b1 = pget()
                    fm_group(N, b1, sco, e_ % 4, 4, lambda kc: zT[:, kc, 0:N], zT_b)
                    b2 = pget()
                    fm_group(N, b2, sgc, el, 8, x_rhs, [xT_b])
                    a1 = nxt("A", NA)
                    S.op("act", ACT(At[a1][:, 0:N], banks[b2][:, 0:N], AF.Sigmoid), reads=[bank_b[b2]], writes=[At_b[a1]])
                    S.op("dve", TT(At[a1][:, 0:N], At[a1][:, 0:N], banks[b1][:, 0:N], ALU.mult), reads=[bank_b[b1], At_b[a1]], writes=[At_b[a1]])
                    b3 = pget()
                    fm_group(N, b3, sao, e_ % 4, 4, lambda kc: OT[:, kc, 0:N], OT_b)
                    b4 = pget()
                    fm_group(N, b4, sga, el, 8, x_rhs, [xT_b])
                    a2 = nxt("A", NA)
                    S.op("act", ACT(At[a2][:, 0:N], banks[b4][:, 0:N], AF.Sigmoid), reads=[bank_b[b4]], writes=[At_b[a2]])
                    S.op("dve", TT(At[a2][:, 0:N], At[a2][:, 0:N], banks[b3][:, 0:N], ALU.mult), reads=[bank_b[b3], At_b[a2]], writes=[At_b[a2]])
                    S.op("pool", TT(mT[:, e_, 0:N], At[a1][:, 0:N], At[a2][:, 0:N], ALU.add), reads=[At_b[a1], At_b[a2]], writes=MX[e_][0:nt])
            mark("gates")

            for tt in range(nt):
                S.dma("sp", DMA(Bt[0:nreal, tt, :], xrows(tt)), B_s[tt], writes=[B_b[tt]])
            so = [[wload(("o", eh, 0)), wload(("o", eh, 1))] for eh in range(2)]

            for tt in range(nt):
                for eh in range(2):
                    bk = pget()
                    S.group("pe", [MM(banks[bk][:, 0:512], mT[:, kc, tt * 128:(tt + 1) * 128], rT(so[eh][kc // 4])[:, kc % 4, :], kc == 0, kc == 7)
                                   for kc in range(8)], reads=[MX[kc][tt] for kc in range(8)] + [ring_b[so[eh][0]], ring_b[so[eh][1]]], writes=[bank_b[bk]])
                    S.op("dve", STT(Bt[:, tt, eh * 512:(eh + 1) * 512], Bt[:, tt, eh * 512:(eh + 1) * 512], ALPHA, banks[bk][:, 0:512], ALU.mult, ALU.add),
                         reads=[bank_b[bk], B_b[tt]], writes=[B_b[tt]])
            mark("ln1")

        def ln1_stats(ck):
            ck["ln1"] = []
            for tt in range(ck["nt"]):
                li = tt
                st, mv, lb = lnst4[li], lnmv4[li], ln4_b[li]
                S.op("dve", lambda e, st=st, tt=tt: e.bn_stats(out=st[:, 0, :], in_=Bt[:, tt, 0:512]), reads=[B_b[tt]], writes=[lb])
                S.op("dve", lambda e, st=st, tt=tt: e.bn_stats(out=st[:, 1, :], in_=Bt[:, tt, 512:1024]), reads=[B_b[tt], lb], writes=[lb])
                S.op("dve", lambda e, st=st, mv=mv: e.bn_aggr(out=mv[:, 0:2], in_=st[:]), reads=[lb], writes=[lb])
            for tt in range(ck["nt"]):
                mv, lb = lnmv4[tt], ln4_b[tt]
                S.op("act", ACT(mv[:, 2:3], mv[:, 1:2], AF.Ln, bias=cst[:, 1:2], scale=1.0), reads=[lb, const_b], writes=[lb])
                S.op("act", ACT(mv[:, 3:4], mv[:, 2:3], AF.Exp, scale=-0.5), reads=[lb], writes=[lb])

        def ln1_norm(ck):
            for tt in range(ck["nt"]):
                mv, lb = lnmv4[tt], ln4_b[tt]
                S.op("dve", TS(Bt[:, tt, :], Bt[:, tt, :], mv[:, 0:1], mv[:, 3:4], ALU.subtract, ALU.mult), reads=[lb, B_b[tt]], writes=[B_b[tt]])
                ln_affine(tt, 0)

        def ln1_cast(ck):
            for tt in range(ck["nt"]):
                S.op("act", ACT(Xb[:, tt, :], Bt[:, tt, :], AF.Copy), reads=[B_b[tt]], writes=[Xb_b[tt]])

        def ln1_tr(ck):
            for tt in range(ck["nt"]):
                bk = pget()
                bv = banks[bk][:].bitcast(BF16).rearrange("p (k t) -> p k t", k=8)
                S.group("pe", [TR(bv[:, kc, :], Xb[:, tt, kc * 128:(kc + 1) * 128]) for kc in range(8)],
                        reads=[Xb_b[tt], const_b], writes=[bank_b[bk]])
                S.op("act", ACT(mT[:, :, tt * 128:(tt + 1) * 128], bv, AF.Copy), reads=[bank_b[bk]], writes=[MX[kc][tt] for kc in range(8)])

        def back(ck, ck_bg):
            nt, nreal, y_o = ck["nt"], ck["nreal"], ck["y_o"]
            N = nt * 128
            x1_rhs = lambda kc: mT[:, kc, 0:N]
            x1_bufs = [MX[kc][t] for kc in range(8) for t in range(nt)]

            for fh in range(2):
                for i in range(8):
                    su = wload(("up", fh * 8 + i))
                    for el in range(2):
                        fl = 2 * i + el
                        bk = pget()
                        fm_group(N, bk, su, el, 8, x1_rhs, x1_bufs)
                        bg(ck_bg, 2)
                        ai = nxt("A", NA)
                        S.op("act", ACT(At[ai][:, 0:N], banks[bk][:, 0:N], AF.Relu), reads=[bank_b[bk]], writes=[At_b[ai]])
                        S.op("pool", TT(hT[:, fl, 0:N], At[ai][:, 0:N], At[ai][:, 0:N], ALU.mult), reads=[At_b[ai]], writes=[hT_b[fl]])
                for eh in range(2):
                    dbk = [pget(hold=True) for _ in range(nt)]
                    for g in range(4):
                        sd = wload(("dn", eh, fh * 4 + g))
                        for tt in range(nt):
                            fns = [MM(banks[dbk[tt]][:, 0:512], hT[:, g * 4 + kl, tt * 128:(tt + 1) * 128], rT(sd)[:, kl, :],
                                      g == 0 and kl == 0, g == 3 and kl == 3) for kl in range(4)]
                            rds = [ring_b[sd]] + hT_b[g * 4:g * 4 + 4]
                            if g == 0:
                                S.group("pe", fns, reads=rds, writes=[bank_b[dbk[tt]]])
                            else:
                                S.group("pe", fns, reads=rds, acc=[bank_b[dbk[tt]]])
                        bg(ck_bg, 3)
                    for tt in range(nt):
                        sl = slice(eh * 512, (eh + 1) * 512)
                        if fh == 0:
                            S.op("dve", STT(Bt[:, tt, sl], Bt[:, tt, sl], ALPHA, banks[dbk[tt]][:, 0:512], ALU.mult, ALU.add),
                                 reads=[bank_b[dbk[tt]], B_b[tt]], writes=[B_b[tt]])
                        else:
                            S.op("dve", TT(Bt[:, tt, sl], Bt[:, tt, sl], banks[dbk[tt]][:, 0:512], ALU.add),
                                 reads=[bank_b[dbk[tt]], B_b[tt]], writes=[B_b[tt]])
                        prel(dbk[tt])
            mark("ffn")
            def ln2_gen():
                for tt in range(nt):
                    for _ in ln_stats_norm_staged(tt):
                        yield
                    ln_affine(tt, 2)
                    pending.append((y_o(tt), Bt[0:nreal, tt, :], [B_b[tt]], Bst_s[tt]))
                    yield

            ln2_task[0] = ln2_gen()
            mark("ln2")

        chunks = [dict(nt=1, nreal=DEC, kt0=8, seq_first=False, conv_out=conv_s, xsrc=xs[0:DEC, :],
                       xrows=lambda tt: xs[0:DEC, :], y_o=lambda tt: y_s[0:DEC, :], k_o=lambda tt: k_s[0:DEC, :],
                       v_o=lambda tt: v_s[0:DEC, :], lf_all=lf_s[0:DEC, :], seq=None)]
        for s_ in range(nseq):
            for c in range(SEQ // CH):
                r0 = s_ * SEQ + c * CH
                rows = lambda tt, r0=r0: slice(r0 + tt * 128, r0 + (tt + 1) * 128)
                chunks.append(dict(nt=4, nreal=128, kt0=4 * c, seq_first=(c == 0),
                                   conv_out=conv_p[s_] if c == SEQ // CH - 1 else None,
                                   xsrc=xp[r0:r0 + CH, :],
                                   xrows=lambda tt, rows=rows: xp[rows(tt), :], y_o=lambda tt, rows=rows: y_p[rows(tt), :],
                                   k_o=lambda tt, rows=rows: k_p[rows(tt), :], v_o=lambda tt, rows=rows: v_p[rows(tt), :],
                                   lf_all=lf_p[r0:r0 + CH, :], seq=s_))

        for e_ in range(4):
            S.group("sp", [DMA(ubuf[:, e_, 0:HIST], cconv[:, e_ * 128:(e_ + 1) * 128].rearrange("t p -> p t"))], sem=hist_s, amount=16, writes=[])
        for e_ in range(4):
            u_b[e_].lw = (hist_s, S.cnt[hist_s])
        S.op("dve", lambda e: e.memset(carry[:, 0, :], 0.0), writes=[carry_b])
        prefetch_x(chunks[0])
        ckv = hT[:, 0:8, :]
        S.dma("pool", DMA(ckv, ck.rearrange("(k p) e -> p k e", p=128)), ck_s, writes=hT_b[0:8])
        cvv = cv.rearrange("(k p) (r a d) -> p k r a d", p=128, r=4, a=2)
        for kt in range(8):
            S.group("pool", [DMA(Vaug[:, kt, :, 0:64], cvv[:, kt, :, 0, :])], sem=cv_s, amount=16)
            S.group("pool", [DMA(Vaug[:, kt, :, 128:192], cvv[:, kt, :, 1, :])], sem=cv_s, amount=16)
        for kt in range(8):
            V_b[kt].lw = (cv_s, S.cnt[cv_s])
        lfc_t = cacc[:, 0, 0:64].rearrange("p (k h) -> p k h", h=8)
        lfc_b = cacc_b[0]
        S.dma("sp", DMA(lfc_t, clf.rearrange("(k p) h -> p k h", p=128)), ck_s2, writes=[lfc_b])
        for kt in range(8):
            bk2 = pget()
            bv = banks[bk2][:].bitcast(BF16)[:, 0:512].rearrange("p (k t) -> p k t", k=4)
            S.group("pe", [TR(bv[:, pr, :], ckv[:, kt, pr * 128:(pr + 1) * 128]) for pr in range(4)],
                    reads=[hT_b[kt], const_b], writes=[bank_b[bk2]])
            S.op("dve", CP(KT[:, :, kt * 128:(kt + 1) * 128], bv), reads=[bank_b[bk2]], writes=[KT_b[kt]])
        cumsum_tiles(0, 8, lambda i: lfc_t[:, i, :], [lfc_b])

        wgrp = [(0, 18), (18, 26), (26, 42), (42, NU)]
        wsc_gs = [S.new_sem(f"wsc{g}") for g in range(len(wgrp))]
        wsc_gb = [Buf(f"wsc{g}") for g in range(len(wgrp))]
        ugrp = {}
        for g, (u0, u1) in enumerate(wgrp):
            for u in range(u0, u1):
                wn, r0, nr, c0, ncol = ULIST[u]
                kcn = nr // 128
                src = W[wn][r0:r0 + nr, c0:c0 + ncol].rearrange("(kc p) e -> p kc e", p=128)
                dst = wsc[u].rearrange("p (kc e) -> p kc e", kc=kcn)
                S.group("pool", [DMA(dst, src)], sem=wsc_gs[g], amount=16)
                ugrp[u] = g
            wsc_gb[g].lw = (wsc_gs[g], S.cnt[wsc_gs[g]])

        try:
            front_a(chunks[0])
            prefetch_x(chunks[1] if len(chunks) > 1 else None)
            front_b(chunks[0])
            mid_qkvf(chunks[0])
            for ci, ckd in enumerate(chunks):
                mid_rest(ckd)
                nxt_ck = chunks[ci + 1] if ci + 1 < len(chunks) else None
                if nxt_ck is not None:
                    front_a(nxt_ck)
                ln1_stats(ckd)
                if nxt_ck is not None:
                    front_b(nxt_ck)
                ln1_norm(ckd)
                if nxt_ck is not None:
                    mid_qkvf(nxt_ck)
                ln1_cast(ckd)
                ln1_tr(ckd)
                prefetch_x(chunks[ci + 2] if ci + 2 < len(chunks) else None)
                back(ckd, nxt_ck)
        except _Stop:
            pass
        ln2_step(100)
        flush_stores()

        if dbg_o is not None:
            dsm = S.new_sem("dbgs")
            S.dma("pool", DMA(dbg_o[:, 0:144], carry[:].rearrange("p a h -> p (a h)")), dsm, reads=[carry_b])
            S.dma("pool", DMA(dbg_o[:, 144:280], negc[:].rearrange("p a h -> p (a h)")), dsm, reads=[negc_b])
            S.final_wait("pool", [(dsm, 32)])
        mark("END")
        if _PHASES is not None:
            _PHASES.extend(_marks)
        fin = [(zs, S.cnt[zs]) for zs in zt_s + kvst_s + Bst_s + cvo_s]
        S.final_wait("pool", fin)
        S.emit()
    return nc


_CACHE = {}


def kernel(x_prompt, x_sample, cache_conv, cache_k, cache_v, cache_logf,
           w_in, b_f, w_dw, b_dw, ln_conv_g, ln_conv_b, w_conv_out, w_attn_out, w_o,
           ln1_g, ln1_b, w_up, w_down, ln2_g, ln2_b, _dbg=None, _nseq=NSEQ, _ncores=NCORES, _raw=False):
    f = lambda a: np.ascontiguousarray(np.asarray(a, dtype=np.float32))
    x_prompt, x_sample = f(x_prompt), f(x_sample)
    B = x_prompt.shape[0]
    nc = build_program(_dbg, _nseq)
    wdw = f(np.asarray(w_dw)[0].T.reshape(4, 128, 31).transpose(1, 0, 2))
    cpar = f(np.stack([np.asarray(b_dw)[0], np.asarray(ln_conv_g)[0], np.asarray(ln_conv_b)[0]]).reshape(3, 4, 128).transpose(2, 1, 0))
    lnp = f(np.stack([np.asarray(ln1_g)[0], np.asarray(ln1_b)[0], np.asarray(ln2_g)[0], np.asarray(ln2_b)[0]]))
    shared = {
        "w_in": f(np.asarray(w_in)[0]), "w_co": f(np.asarray(w_conv_out)[0]), "w_ao": f(np.asarray(w_attn_out)[0]),
        "w_o": f(np.asarray(w_o)[0]), "w_up": f(np.asarray(w_up)[0]), "w_down": f(np.asarray(w_down)[0]),
        "b_f": f(np.asarray(b_f)[0:1]), "wdw": wdw, "cpar": cpar, "lnp": lnp,
    }
    cache_conv, cache_k, cache_v, cache_logf = f(cache_conv), f(cache_k), f(cache_v), f(cache_logf)
    in_maps = []
    for c in range(_ncores):
        m = dict(shared)
        m["xp"] = x_prompt[NSEQ * c:NSEQ * c + _nseq].reshape(_nseq * SEQ, D)
        m["xs"] = x_sample[c]
        m["cconv"] = cache_conv[0, c]
        m["ck"] = cache_k[0, c].reshape(PAST, 512)
        m["cv"] = cache_v[0, c].reshape(PAST, 512)
        m["clf"] = cache_logf[0, c]
        in_maps.append(m)
    res = run_bass_kernel_spmd(nc, in_maps, core_ids=list(range(_ncores)))
    if _raw:
        return res.results
    R = res.results
    cat = lambda k: np.concatenate([np.asarray(r[k]) for r in R], axis=0)
    y_prompt = cat("y_p").reshape(B, SEQ, D)
    y_sample = np.stack([np.asarray(r["y_s"]) for r in R]).reshape(NCORES, DEC, D)
    conv_prompt = cat("conv_p").reshape(1, B, HIST, 512)
    k_prompt = cat("k_p").reshape(1, B, SEQ, 8, 64)
    v_prompt = cat("v_p").reshape(1, B, SEQ, 8, 64)
    logf_prompt = cat("lf_p").reshape(1, B, SEQ, 8)
    conv_sample = np.stack([np.asarray(r["conv_s"]) for r in R]).reshape(1, NCORES, HIST, 512)
    k_sample = np.stack([np.asarray(r["k_s"]) for r in R]).reshape(1, NCORES, DEC, 8, 64)
    v_sample = np.stack([np.asarray(r["v_s"]) for r in R]).reshape(1, NCORES, DEC, 8, 64)
    logf_sample = np.stack([np.asarray(r["lf_s"]) for r in R]).reshape(1, NCORES, DEC, 8)
    return (y_prompt, y_sample, conv_prompt, k_prompt, v_prompt, logf_prompt,
            conv_sample, k_sample, v_sample, logf_sample)
```

```python
import numpy as np
from contextlib import ExitStack
import concourse.bass as bass
import concourse.mybir as mybir
from concourse.bass_utils import run_bass_kernel_spmd

F32 = mybir.dt.float32
BF16 = mybir.dt.bfloat16
ALU = mybir.AluOpType
AF = mybir.ActivationFunctionType
ENGS = ("pe", "act", "dve", "pool", "sp")

NCORES = 8
D = 1024
SEQ = 2048
NSEQ = 4
CH = 512
PAST = 1024
DEC = 32
HIST = 30
DIN = 4616
OFF_B, OFF_Q, OFF_K, OFF_V, OFF_F, OFF_GC, OFF_GA = 512, 1024, 1536, 2048, 2560, 2568, 3592
ALPHA = float(2.0 ** 0.25)
EPS = 1e-5
NRING = 8
NEG = -30000.0
SPACE = 5


class Buf:
    __slots__ = ("name", "lw", "rd", "excl")

    def __init__(self, name, excl=False):
        self.name = name
        self.lw = None
        self.rd = []
        self.excl = excl


class Sched:
    def __init__(self, nc, es):
        self.nc = nc
        self.es = es
        self.ops = {e: [] for e in ENGS}
        self.sems = []
        self.cnt = []
        self.esem = {}
        for e in ENGS:
            self.esem[e] = self.new_sem("e_" + e)
        self.seen = {e: {} for e in ENGS}

    def new_sem(self, name):
        h = self.es.enter_context(self.nc.semaphore(name))
        self.sems.append(h)
        self.cnt.append(0)
        return len(self.sems) - 1

    def _waits(self, eng, toks):
        best = {}
        for (s, v) in toks:
            if v > best.get(s, 0):
                best[s] = v
        out = []
        seen = self.seen[eng]
        for s, v in best.items():
            if seen.get(s, 0) >= v:
                continue
            seen[s] = v
            out.append((s, v))
        return out

    def group(self, eng, fns, reads=(), writes=(), acc=(), sem=None, amount=1):
        toks = set()
        for b in reads:
            if b.lw is not None:
                toks.add(b.lw)
            if b.excl:
                toks.update(b.rd)
        for b in writes:
            if b.lw is not None:
                toks.add(b.lw)
            toks.update(b.rd)
        waits = self._waits(eng, toks)
        s = self.esem[eng] if sem is None else sem
        self.cnt[s] += amount
        tok = (s, self.cnt[s])
        n = len(fns)
        for i, fn in enumerate(fns):
            self.ops[eng].append((waits if i == 0 else (), fn, (s, amount) if i == n - 1 else None))
        for b in writes:
            b.lw = tok
            b.rd = []
        for b in acc:
            b.lw = tok
            b.rd = []
        for b in reads:
            if b not in writes:
                b.rd.append(tok)
        return tok

    def op(self, eng, fn, reads=(), writes=()):
        return self.group(eng, [fn], reads, writes)

    def dma(self, eng, fn, sem, reads=(), writes=()):
        return self.group(eng, [fn], reads, writes, sem=sem, amount=16)

    def final_wait(self, eng, toks):
        waits = self._waits(eng, toks)
        self.ops[eng].append((waits, None, None))

    def emit(self):
        nc = self.nc
        sems = self.sems

        def run(eng_name, e):
            for waits, fn, inc in self.ops[eng_name]:
                for (s, v) in waits:
                    e.wait_ge(sems[s], v)
                if fn is None:
                    continue
                ins = fn(e)
                if inc is not None:
                    ins.then_inc(sems[inc[0]], inc[1])

        with nc.Block() as block:
            @block.tensor
            def _(e):
                run("pe", e)

            @block.scalar
            def _(e):
                run("act", e)

            @block.vector
            def _(e):
                run("dve", e)

            @block.gpsimd
            def _(e):
                run("pool", e)

            @block.sync
            def _(e):
                run("sp", e)


def _units():
    U = {}
    lst = []

    def add(key, spec):
        U[key] = len(lst)
        lst.append(spec)

    for i in range(2):
        add(("a", i), ("w_in", 0, 1024, 256 * i, 256))
    for i in range(2):
        add(("b", i), ("w_in", 0, 1024, OFF_B + 256 * i, 256))
    for i in range(2):
        add(("q", i), ("w_in", 0, 1024, OFF_Q + 256 * i, 256))
    for hh in range(2):
        add(("k", hh), ("w_in", 512 * hh, 512, OFF_K, 512))
    for hh in range(2):
        add(("v", hh), ("w_in", 512 * hh, 512, OFF_V, 512))
    for i in range(4):
        add(("gc", i), ("w_in", 0, 1024, OFF_GC + 256 * i, 256))
    for i in range(4):
        add(("ga", i), ("w_in", 0, 1024, OFF_GA + 256 * i, 256))
    for eh in range(2):
        add(("co", eh), ("w_co", 0, 512, 512 * eh, 512))
    for eh in range(2):
        add(("ao", eh), ("w_ao", 0, 512, 512 * eh, 512))
    for eh in range(2):
        for hh in range(2):
            add(("o", eh, hh), ("w_o", 512 * hh, 512, 512 * eh, 512))
    for fp in range(16):
        add(("up", fp), ("w_up", 0, 1024, 256 * fp, 256))
    for eh in range(2):
        for g in range(8):
            add(("dn", eh, g), ("w_down", 512 * g, 512, 512 * eh, 512))
    return U, lst


_PHASES = None


class _Stop(Exception):
    pass


def build_program(dbg=None, nseq=NSEQ):
    nc = bass.Bass("TRN2", target_bir_lowering=False)

    _marks = []

    def mark(name):
        if _PHASES is not None:
            _marks.append((name, {e: sum(1 for o in S_ref[0].ops[e] if o[1] is not None) for e in ENGS}))
        if dbg is not None and name == dbg:
            raise _Stop()

    S_ref = [None]

    def din(name, shape):
        return nc.dram_tensor(name, list(shape), F32, kind="ExternalInput").ap()

    def dout(name, shape):
        return nc.dram_tensor(name, list(shape), F32, kind="ExternalOutput").ap()

    xp = din("xp", [nseq * SEQ, D])
    xs = din("xs", [DEC, D])
    cconv = din("cconv", [HIST, 512])
    ck = din("ck", [PAST, 512])
    cv = din("cv", [PAST, 512])
    clf = din("clf", [PAST, 8])
    W = {
        "w_in": din("w_in", [D, DIN]),
        "w_co": din("w_co", [512, D]),
        "w_ao": din("w_ao", [512, D]),
        "w_o": din("w_o", [D, D]),
        "w_up": din("w_up", [D, 4096]),
        "w_down": din("w_down", [4096, D]),
    }
    b_f = din("b_f", [1, 8])
    wdw_d = din("wdw", [128, 4, 31])
    cpar_d = din("cpar", [128, 4, 3])
    lnp_d = din("lnp", [4, D])

    y_p = dout("y_p", [nseq * SEQ, D])
    y_s = dout("y_s", [DEC, D])
    conv_p = dout("conv_p", [nseq, HIST, 512])
    k_p = dout("k_p", [nseq * SEQ, 512])
    v_p = dout("v_p", [nseq * SEQ, 512])
    lf_p = dout("lf_p", [nseq * SEQ, 8])
    conv_s = dout("conv_s", [HIST, 512])
    k_s = dout("k_s", [DEC, 512])
    v_s = dout("v_s", [DEC, 512])
    lf_s = dout("lf_s", [DEC, 8])
    dbg_o = dout("dbg_o", [128, 35 * 8]) if dbg == "DUMP" else None

    UIDX, ULIST = _units()
    NU = len(ULIST)
    wsc = nc.dram_tensor("wsc", [NU, 128, 2048], BF16, kind="Internal").ap()

    with ExitStack() as es:
        S = Sched(nc, es)
        S_ref[0] = S
        es.enter_context(nc.allow_non_contiguous_dma(reason="tiny transposing state DMAs"))

        def sb(name, shape, dt):
            return es.enter_context(nc.sbuf_tensor("s_" + name, list(shape), dt))

        ring = [sb(f"ring{i}", [128, 2048], BF16) for i in range(NRING)]
        ring_b = [Buf(f"ring{i}") for i in range(NRING)]
        ring_s = [S.new_sem(f"ring{i}") for i in range(NRING)]
        Bt = sb("Bt", [128, 4, D], F32)
        B_b = [Buf(f"B{i}") for i in range(4)]
        B_s = [S.new_sem(f"Bld{i}") for i in range(4)]
        Bst_s = [S.new_sem(f"Bst{i}") for i in range(4)]
        Xb = sb("Xb", [128, 4, D], BF16)
        Xb_b = [Buf(f"Xb{i}") for i in range(4)]
        Xb_s = S.new_sem("Xb")
        xT = sb("xT", [128, 8, CH], BF16)
        xT_b = Buf("xT")
        ubuf = sb("ubuf", [128, 4, HIST + CH], F32)
        u_b = [Buf(f"u{i}") for i in range(4)]
        cacc = sb("cacc", [128, 4, CH], F32)
        cacc_b = [Buf(f"cacc{i}") for i in range(4)]
        ybf = sb("ybf", [128, 4, CH], BF16)
        ybf_b = [Buf(f"ybf{i}") for i in range(4)]
        zT = sb("zT", [128, 4, CH], BF16)
        zT_b = [Buf(f"zT{i}") for i in range(4)]
        Qpad = sb("Qpad", [128, 8, CH], BF16)
        Q_b = [Buf(f"Q{i}") for i in range(8)]
        KT = sb("KT", [128, 4, SEQ], BF16)
        KT_b = [Buf(f"KT{i}") for i in range(16)]
        Vaug = sb("Vaug", [128, 16, 4, 192], BF16)
        V_b = [Buf(f"V{i}") for i in range(16)]
        negc = sb("negc", [128, 17, 8], F32)
        negc_b = Buf("negc")
        carry = sb("carry", [128, 18, 8], F32)
        carry_b = Buf("carry")
        NPT = 4
        PT = [sb(f"PT{i}", [128, CH], BF16) for i in range(NPT)]
        PT_b = [Buf(f"PT{i}") for i in range(NPT)]
        OT = sb("OT", [128, 4, CH], BF16)
        OT_b = [Buf(f"OT{i}") for i in range(4)]
        den = [sb(f"den{i}", [128, CH], F32) for i in range(1)]
        den_b = [Buf(f"den{i}") for i in range(1)]
        NA = 3
        At = [sb(f"At{i}", [128, CH], F32) for i in range(NA)]
        At_b = [Buf(f"At{i}") for i in range(NA)]
        mT = sb("mT", [128, 8, CH], BF16)
        MX = [[Buf(f"mT{i}_{t}") for t in range(4)] for i in range(8)]
        cvst = sb("cvst", [128, 4, HIST], F32)
        cvst_b = [Buf(f"cvst{i}") for i in range(4)]
        rstd_t = sb("rstd_t", [128, CH], F32)
        rstd_b = Buf("rstd")
        hT = sb("hT", [128, 16, CH], BF16)
        hT_b = [Buf(f"hT{i}") for i in range(16)]
        NKV = 2
        kvst = [sb(f"kvst{i}", [128, 512], F32) for i in range(NKV)]
        kvst_b = [Buf(f"kvst{i}") for i in range(NKV)]
        kvst_s = [S.new_sem(f"kvst{i}") for i in range(NKV)]
        kb = [sb(f"kb{i}", [128, 512], BF16) for i in range(2)]
        kb_b = [Buf(f"kb{i}") for i in range(2)]
        zt = [sb(f"zt{i}", [128, 3, 4, 8], F32) for i in range(1)]
        zt_b = [Buf(f"zt{i}") for i in range(1)]
        zt_s = [S.new_sem(f"zt{i}") for i in range(1)]
        lnst = [sb(f"lnst{i}", [128, 2, 6], F32) for i in range(4)]
        lnmv = [sb(f"lnmv{i}", [128, 4], F32) for i in range(4)]
        ln_b = [Buf(f"ln{i}") for i in range(4)]
        lnst4, lnmv4, ln4_b = lnst, lnmv, ln_b
        ident = sb("ident", [128, 128], BF16)
        maskt = sb("maskt", [128, 128], BF16)
        Uf = sb("Uf", [128, 128], F32)
        onesf = sb("onesf", [128, 128], F32)
        onesM = sb("onesM", [128, 128], BF16)
        negones = sb("negones", [128, 128], BF16)
        Dq = [sb(f"Dq{i}", [128, 4, 128], BF16) for i in range(2)]
        Dq_b = [Buf(f"Dq{i}") for i in range(2)]
        cst = sb("cst", [128, 4], F32)
        bfb = sb("bfb", [128, 8], F32)
        Wf = sb("Wf", [128, 8, 8], BF16)
        wdw = sb("wdw", [128, 4, 31], F32)
        cpar = sb("cpar", [128, 4, 3], F32)
        lnp = sb("lnp", [128, 4, D], F32)
        const_b = Buf("const")
        cd_b = Buf("cdma")
        const_s = S.new_sem("const")
        wf_s = S.new_sem("wfc")
        hist_s = S.new_sem("hist")
        ck_s = S.new_sem("ck")
        ck_s2 = S.new_sem("ck2")
        cv_s = S.new_sem("cv")
        cvo_s = [S.new_sem(f"cvo{i}") for i in range(4)]
        wsc_b = Buf("wsc")
        wsc_s = S.new_sem("wsc")
        out_s = S.new_sem("outs")
        out_cnt = [0]

        banks = [es.enter_context(nc.psum_tensor(f"bank{i}", [128, 512], F32)) for i in range(8)]
        bank_b = [Buf(f"bank{i}", excl=True) for i in range(8)]
        pstate = {"i": 0, "held": set()}

        def pget(hold=False):
            for _ in range(16):
                i = pstate["i"]
                pstate["i"] = (i + 1) % 8
                if i in pstate["held"]:
                    continue
                if hold:
                    pstate["held"].add(i)
                return i
            raise RuntimeError("no psum bank")

        def prel(i):
            pstate["held"].discard(i)

        def MM(out, lhsT, rhs, start, stop):
            return lambda e: e.matmul(out, lhsT=lhsT, rhs=rhs, start=start, stop=stop)

        def TR(out, in_):
            return lambda e: e.transpose(out=out, in_=in_, identity=ident[:])

        def ACT(out, in_, func, **kw):
            return lambda e: e.activation(out=out, in_=in_, func=func, **kw)

        def TT(out, in0, in1, op):
            return lambda e: e.tensor_tensor(out=out, in0=in0, in1=in1, op=op)

        def TS(out, in0, s1, s2, op0, op1=None):
            if op1 is None:
                return lambda e: e.tensor_scalar(out=out, in0=in0, scalar1=s1, scalar2=None, op0=op0)
            return lambda e: e.tensor_scalar(out=out, in0=in0, scalar1=s1, scalar2=s2, op0=op0, op1=op1)

        def STT(out, in0, scalar, in1, op0, op1):
            return lambda e: e.scalar_tensor_tensor(out=out, in0=in0, scalar=scalar, in1=in1, op0=op0, op1=op1)

        def CP(out, in_):
            return lambda e: e.tensor_copy(out=out, in_=in_)

        def DMA(out, in_):
            return lambda e: e.dma_start(out=out, in_=in_)

        def store(out, in_, reads, sem):
            return S.dma("pool", DMA(out, in_), sem, reads=reads)

        def cdma(eng, out, in_):
            S.group(eng, [DMA(out, in_)], sem=const_s, amount=16)

        cdma("sp", bfb[:], b_f.partition_broadcast(128).rearrange("p a h -> p (a h)"))
        cdma("sp", wdw[:], wdw_d)
        cdma("sp", cpar[:], cpar_d)
        for i in range(4):
            cdma("sp", lnp[:, i, :], lnp_d[i:i + 1, :].partition_broadcast(128).rearrange("p a d -> p (a d)"))
        cd_b.lw = (const_s, S.cnt[const_s])
        wf_b = Buf("wf")
        S.dma("pool", DMA(Wf[:], W["w_in"][:, OFF_F:OFF_F + 8].rearrange("(kc p) e -> p kc e", p=128)), wf_s, writes=[wf_b])

        cb = [const_b]
        S.op("pool", lambda e: e.memset(ident[:], 0.0), writes=cb)
        S.op("pool", lambda e: e.affine_select(out=ident[:], in_=ident[:], pattern=[[-1, 128]], compare_op=ALU.not_equal,
                                               fill=1.0, base=0, channel_multiplier=1), writes=cb)
        S.op("pool", lambda e: e.memset(maskt[:], 0.0), writes=cb)
        S.op("pool", lambda e: e.affine_select(out=maskt[:], in_=maskt[:], pattern=[[1, 128]], compare_op=ALU.is_ge,
                                               fill=NEG, base=0, channel_multiplier=-1), writes=cb)
        S.op("pool", lambda e: e.memset(Uf[:], 1.0), writes=cb)
        S.op("pool", lambda e: e.affine_select(out=Uf[:], in_=Uf[:], pattern=[[1, 128]], compare_op=ALU.is_ge,
                                               fill=0.0, base=0, channel_multiplier=-1), writes=cb)
        S.op("pool", lambda e: e.memset(onesf[:], 1.0), writes=cb)
        S.op("pool", lambda e: e.memset(onesM[:], 1.0 / 512.0), writes=cb)
        S.op("pool", lambda e: e.memset(negones[:], -1.0), writes=cb)
        S.op("pool", lambda e: e.memset(cst[:, 0:1], 1.0), writes=cb)
        S.op("pool", lambda e: e.memset(cst[:, 1:2], EPS), writes=cb)
        S.op("pool", lambda e: e.memset(cst[:, 2:4], 0.0), writes=cb)
        S.op("pool", lambda e: e.memset(Qpad[:], 0.0), writes=Q_b)
        S.op("pool", lambda e: e.memset(Vaug[:, :, :, 64:128], 1.0), writes=V_b)
        S.op("pool", lambda e: e.memset(Xb[:, 0, :], 0.0), writes=[Xb_b[0]])
        S.op("pool", lambda e: e.memset(Bt[:, 0, :], 0.0), writes=[B_b[0]])
        S.op("pool", lambda e: e.memset(zt[0][:], 0.0), writes=[zt_b[0]])

        rpos = [0]

        def wload(key):
            slot = rpos[0] % NRING
            rpos[0] += 1
            S.dma("sp", DMA(ring[slot][:], wsc[UIDX[key]]), ring_s[slot], reads=[wsc_gb[ugrp[UIDX[key]]]], writes=[ring_b[slot]])
            return slot

        def rF(slot):
            return ring[slot][:].rearrange("p (k e) -> p k e", k=8)

        def rT(slot):
            return ring[slot][:].rearrange("p (k e) -> p k e", k=4)

        rr = {"A": 0, "kv": 0, "kb": 0, "pt": 0, "zt": 0, "ln": 0, "den": 0, "x1b": 0}

        def nxt(name, n):
            i = rr[name]
            rr[name] = (i + 1) % n
            return i

        def cumsum_tiles(kt_s, n, lf_fn, lf_bufs):
            bk = pget()
            fns = []
            for i in range(n):
                fns.append(MM(banks[bk][:, i * 8:(i + 1) * 8], Uf[:], lf_fn(i), True, i == 0))
                for j in range(i):
                    fns.append(MM(banks[bk][:, i * 8:(i + 1) * 8], onesf[:], lf_fn(j), False, j == i - 1))
                for j in range(i + 1):
                    fns.append(MM(banks[bk][:, (n + i) * 8:(n + i + 1) * 8], onesf[:], lf_fn(j), j == 0, j == i))
            S.group("pe", fns, reads=list(lf_bufs) + [const_b], writes=[bank_b[bk]])
            cbc = carry[:, kt_s:kt_s + 1, :].to_broadcast([128, n, 8])
            S.op("dve", STT(negc[:, kt_s:kt_s + n, :], banks[bk][:, 0:n * 8].rearrange("p (t h) -> p t h", h=8), -1.0, cbc, ALU.mult, ALU.subtract),
                 reads=[bank_b[bk], carry_b], writes=[negc_b])
            S.op("dve", TT(carry[:, kt_s + 1:kt_s + n + 1, :], banks[bk][:, n * 8:2 * n * 8].rearrange("p (t h) -> p t h", h=8), cbc, ALU.add),
                 reads=[bank_b[bk], carry_b], writes=[carry_b])

        def prefetch_x(ck):
            if ck is None:
                return
            if ck["nt"] == 1:
                S.dma("pool", DMA(Xb[0:ck["nreal"], 0, :], ck["xsrc"]), Xb_s, writes=[Xb_b[0]])
            else:
                S.dma("pool", DMA(Xb[:, :, :], ck["xsrc"].rearrange("(t p) d -> p t d", p=128)), Xb_s, writes=Xb_b)

        def ln_stats_norm(tt):
            li = nxt("ln", 2)
            st, mv, lb = lnst[li], lnmv[li], ln_b[li]
            S.op("dve", lambda e: e.bn_stats(out=st[:, 0, :], in_=Bt[:, tt, 0:512]), reads=[B_b[tt]], writes=[lb])
            S.op("dve", lambda e: e.bn_stats(out=st[:, 1, :], in_=Bt[:, tt, 512:1024]), reads=[B_b[tt], lb], writes=[lb])
            S.op("dve", lambda e: e.bn_aggr(out=mv[:, 0:2], in_=st[:]), reads=[lb], writes=[lb])
            S.op("act", ACT(mv[:, 2:3], mv[:, 1:2], AF.Ln, bias=cst[:, 1:2], scale=1.0), reads=[lb, const_b], writes=[lb])
            S.op("act", ACT(mv[:, 3:4], mv[:, 2:3], AF.Exp, scale=-0.5), reads=[lb], writes=[lb])
            S.op("dve", TS(Bt[:, tt, :], Bt[:, tt, :], mv[:, 0:1], mv[:, 3:4], ALU.subtract, ALU.mult), reads=[lb, B_b[tt]], writes=[B_b[tt]])

        def ln_stats_norm_staged(tt):
            li = nxt("ln", 2)
            st, mv, lb = lnst[li], lnmv[li], ln_b[li]
            S.op("dve", lambda e: e.bn_stats(out=st[:, 0, :], in_=Bt[:, tt, 0:512]), reads=[B_b[tt]], writes=[lb])
            S.op("dve", lambda e: e.bn_stats(out=st[:, 1, :], in_=Bt[:, tt, 512:1024]), reads=[B_b[tt], lb], writes=[lb])
            S.op("dve", lambda e: e.bn_aggr(out=mv[:, 0:2], in_=st[:]), reads=[lb], writes=[lb])
            for _ in range(SPACE):
                yield
            S.op("act", ACT(mv[:, 2:3], mv[:, 1:2], AF.Ln, bias=cst[:, 1:2], scale=1.0), reads=[lb, const_b], writes=[lb])
            S.op("act", ACT(mv[:, 3:4], mv[:, 2:3], AF.Exp, scale=-0.5), reads=[lb], writes=[lb])
            for _ in range(SPACE):
                yield
            S.op("dve", TS(Bt[:, tt, :], Bt[:, tt, :], mv[:, 0:1], mv[:, 3:4], ALU.subtract, ALU.mult), reads=[lb, B_b[tt]], writes=[B_b[tt]])

        def ln_affine(tt, gi):
            S.op("dve", TT(Bt[:, tt, :], Bt[:, tt, :], lnp[:, gi, :], ALU.mult), reads=[B_b[tt], cd_b], writes=[B_b[tt]])
            S.op("dve", TT(Bt[:, tt, :], Bt[:, tt, :], lnp[:, gi + 1, :], ALU.add), reads=[B_b[tt], cd_b], writes=[B_b[tt]])


        def fm_group(N, bk, slot, el, kn, rhs_fn, rbufs):
            if kn == 8:
                wv = rF(slot)
                fns = [MM(banks[bk][:, 0:N], wv[:, kc, el * 128:(el + 1) * 128], rhs_fn(kc), kc == 0, kc == 7) for kc in range(8)]
            else:
                wv = rT(slot)
                fns = [MM(banks[bk][:, 0:N], wv[:, kc, el * 128:(el + 1) * 128], rhs_fn(kc), kc == 0, kc == 3) for kc in range(4)]
            S.group("pe", fns, reads=[ring_b[slot]] + rbufs, writes=[bank_b[bk]])

        pending = []
        ln2_task = [None]

        def ln2_step(n):
            for _ in range(n):
                if ln2_task[0] is None:
                    return
                try:
                    next(ln2_task[0])
                except StopIteration:
                    ln2_task[0] = None
                    return

        def flush_stores():
            while pending:
                o_, i_, r_, s_ = pending.pop(0)
                store(o_, i_, r_, s_)

        def bg(ck, n):
            if ck is None or ck.get("cg") is None:
                return
            for _ in range(n):
                try:
                    next(ck["cg"])
                except StopIteration:
                    ck["cg"] = None
                    return

        def front_a(ck):
            if ck["seq"] is not None and ck["seq_first"]:
                for e_ in range(4):
                    S.op("pool", lambda e, e_=e_: e.memset(ubuf[:, e_, 0:HIST], 0.0), writes=[u_b[e_]])
            nt, nreal, conv_out = ck["nt"], ck["nreal"], ck["conv_out"]
            N = nt * 128
            treal = (nt - 1) * 128 + nreal

            for tt in range(nt):
                bk = pget()
                bv = banks[bk][:].bitcast(BF16).rearrange("p (k t) -> p k t", k=8)
                S.group("pe", [TR(bv[:, kc, :], Xb[:, tt, kc * 128:(kc + 1) * 128]) for kc in range(8)],
                        reads=[Xb_b[tt], const_b], writes=[bank_b[bk]])
                S.op("act", ACT(xT[:, :, tt * 128:(tt + 1) * 128], bv, AF.Copy), reads=[bank_b[bk]], writes=[xT_b])
            mark("xT")

        def front_b(ck):
            nt, nreal, conv_out = ck["nt"], ck["nreal"], ck["conv_out"]
            N = nt * 128
            treal = (nt - 1) * 128 + nreal
            x_rhs = lambda kc: xT[:, kc, 0:N]

            for i in range(2):
                sa = wload(("a", i))
                abk = []
                for el in range(2):
                    bk = pget(hold=True)
                    fm_group(N, bk, sa, el, 8, x_rhs, [xT_b])
                    abk.append(bk)
                sbq = wload(("b", i))
                for el in range(2):
                    e_ = 2 * i + el
                    bk = pget()
                    fm_group(N, bk, sbq, el, 8, x_rhs, [xT_b])
                    ai = nxt("A", NA)
                    S.op("act", ACT(At[ai][:, 0:N], banks[bk][:, 0:N], AF.Sigmoid), reads=[bank_b[bk]], writes=[At_b[ai]])
                    S.op("dve", TT(ubuf[:, e_, HIST:HIST + N], banks[abk[el]][:, 0:N], At[ai][:, 0:N], ALU.mult),
                         reads=[bank_b[abk[el]], At_b[ai]], writes=[u_b[e_]])
                    prel(abk[el])
            mark("glu")

            def conv_gen():
                for e_ in range(4):
                    S.op("dve", TS(cacc[:, e_, 0:N], ubuf[:, e_, 0:N], wdw[:, e_, 0:1], cpar[:, e_, 0:1], ALU.mult, ALU.add),
                         reads=[u_b[e_], cd_b], writes=[cacc_b[e_]])
                    yield
                    for j in range(1, 31):
                        S.op("dve", STT(cacc[:, e_, 0:N], ubuf[:, e_, j:j + N], wdw[:, e_, j:j + 1], cacc[:, e_, 0:N], ALU.mult, ALU.add),
                             reads=[u_b[e_]], writes=[cacc_b[e_]])
                        yield
                if conv_out is not None:
                    for e_ in range(4):
                        S.op("dve", CP(cvst[:, e_, :], ubuf[:, e_, treal:treal + HIST]), reads=[u_b[e_]], writes=[cvst_b[e_]])
                        store(conv_out[:, e_ * 128:(e_ + 1) * 128].rearrange("t p -> p t"), cvst[:, e_, :], [cvst_b[e_]], cvo_s[e_])
                for e_ in range(4):
                    S.op("pool", CP(ubuf[:, e_, 0:HIST], ubuf[:, e_, N:N + HIST]), reads=[], writes=[u_b[e_]])
                yield
                for e_ in range(4):
                    S.op("pool", CP(ybf[:, e_, 0:N], cacc[:, e_, 0:N]), reads=[cacc_b[e_]], writes=[ybf_b[e_]])
                for _ in range(SPACE):
                    yield
                bm = pget(hold=True)
                S.group("pe", [MM(banks[bm][:, 0:N], onesM[:], ybf[:, e_, 0:N], e_ == 0, e_ == 3) for e_ in range(4)],
                        reads=ybf_b + [const_b], writes=[bank_b[bm]])
                for _ in range(SPACE):
                    yield
                for e_ in range(4):
                    S.op("dve", TT(cacc[:, e_, 0:N], cacc[:, e_, 0:N], banks[bm][:, 0:N], ALU.subtract), reads=[bank_b[bm], cacc_b[e_]], writes=[cacc_b[e_]])
                prel(bm)
                for _ in range(SPACE):
                    yield
                for e_ in range(4):
                    S.op("pool", TT(ybf[:, e_, 0:N], cacc[:, e_, 0:N], cacc[:, e_, 0:N], ALU.mult), reads=[cacc_b[e_]], writes=[ybf_b[e_]])
                for _ in range(SPACE):
                    yield
                bv_ = pget(hold=True)
                S.group("pe", [MM(banks[bv_][:, 0:N], onesM[:], ybf[:, e_, 0:N], e_ == 0, e_ == 3) for e_ in range(4)],
                        reads=ybf_b + [const_b], writes=[bank_b[bv_]])
                for _ in range(SPACE):
                    yield
                S.op("act", ACT(rstd_t[:, 0:N], banks[bv_][:, 0:N], AF.Ln, bias=cst[:, 1:2], scale=1.0), reads=[bank_b[bv_], const_b], writes=[rstd_b])
                S.op("act", ACT(rstd_t[:, 0:N], rstd_t[:, 0:N], AF.Exp, scale=-0.5), reads=[rstd_b], writes=[rstd_b])
                prel(bv_)
                for _ in range(SPACE):
                    yield
                for e_ in range(4):
                    S.op("dve", TT(cacc[:, e_, 0:N], cacc[:, e_, 0:N], rstd_t[:, 0:N], ALU.mult), reads=[cacc_b[e_], rstd_b], writes=[cacc_b[e_]])
                for _ in range(SPACE):
                    yield
                for e_ in range(4):
                    S.op("act", ACT(zT[:, e_, 0:N], cacc[:, e_, 0:N], AF.Silu, bias=cpar[:, e_, 2:3], scale=cpar[:, e_, 1:2]),
                         reads=[cacc_b[e_], cd_b], writes=[zT_b[e_]])
                yield

            ck["cg"] = conv_gen()

        def mid_qkvf(ck):
            nt, nreal, kt0, seq_first = ck["nt"], ck["nreal"], ck["kt0"], ck["seq_first"]
            xrows, k_o, v_o, lf_all = ck["xrows"], ck["k_o"], ck["v_o"], ck["lf_all"]
            N = nt * 128
            x_rhs = lambda kc: xT[:, kc, 0:N]

            for i in range(2):
                sq = wload(("q", i))
                for el in range(2):
                    pr = 2 * i + el
                    bk = pget()
                    fm_group(N, bk, sq, el, 8, x_rhs, [xT_b])
                    S.op("act", ACT(Qpad[0:64, 2 * pr, 0:N], banks[bk][0:64, 0:N], AF.Copy, scale=0.125),
                         reads=[bank_b[bk]], writes=[Q_b[2 * pr]])
                    S.op("act", ACT(Qpad[64:128, 2 * pr + 1, 0:N], banks[bk][64:128, 0:N], AF.Copy, scale=0.125),
                         reads=[bank_b[bk]], writes=[Q_b[2 * pr + 1]])
            mark("q")

            sk = [wload(("k", 0)), wload(("k", 1))]
            kbi = {}
            for tt in range(nt + 1):
                if tt < nt:
                    bk = pget()
                    S.group("pe", [MM(banks[bk][:, 0:512], xT[:, kc, tt * 128:(tt + 1) * 128], rT(sk[kc // 4])[:, kc % 4, :], kc == 0, kc == 7)
                                   for kc in range(8)], reads=[xT_b, ring_b[sk[0]], ring_b[sk[1]]], writes=[bank_b[bk]])
                    si = nxt("kv", NKV)
                    S.op("act", ACT(kvst[si][:], banks[bk][:], AF.Copy), reads=[bank_b[bk]], writes=[kvst_b[si]])
                    ki = nxt("kb", 2)
                    kbi[tt] = ki
                    S.op("act", ACT(kb[ki][:], banks[bk][:], AF.Copy), reads=[bank_b[bk]], writes=[kb_b[ki]])
                    store(k_o(tt), kvst[si][0:nreal, :], [kvst_b[si]], kvst_s[si])
                if tt >= 1:
                    t2 = tt - 1
                    kt = kt0 + t2
                    ki = kbi[t2]
                    bk2 = pget()
                    bv = banks[bk2][:].bitcast(BF16)[:, 0:512].rearrange("p (k t) -> p k t", k=4)
                    S.group("pe", [TR(bv[:, pr, :], kb[ki][:, pr * 128:(pr + 1) * 128]) for pr in range(4)],
                            reads=[kb_b[ki], const_b], writes=[bank_b[bk2]])
                    S.op("act", ACT(KT[:, :, kt * 128:(kt + 1) * 128], bv, AF.Copy), reads=[bank_b[bk2]], writes=[KT_b[kt]])
            mark("k")

            sv = [wload(("v", 0)), wload(("v", 1))]
            for tt in range(nt):
                kt = kt0 + tt
                bk = pget()
                S.group("pe", [MM(banks[bk][:, 0:512], xT[:, kc, tt * 128:(tt + 1) * 128], rT(sv[kc // 4])[:, kc % 4, :], kc == 0, kc == 7)
                               for kc in range(8)], reads=[xT_b, ring_b[sv[0]], ring_b[sv[1]]], writes=[bank_b[bk]])
                si = nxt("kv", NKV)
                S.op("act", ACT(kvst[si][:], banks[bk][:], AF.Copy), reads=[bank_b[bk]], writes=[kvst_b[si]])
                store(v_o(tt), kvst[si][0:nreal, :], [kvst_b[si]], kvst_s[si])
                vv = banks[bk][:].rearrange("p (r a d) -> p r a d", r=4, a=2)
                S.op("act", ACT(Vaug[:, kt, :, 0:64], vv[:, :, 0, :], AF.Copy), reads=[bank_b[bk]], writes=[V_b[kt]])
                S.op("act", ACT(Vaug[:, kt, :, 128:192], vv[:, :, 1, :], AF.Copy), reads=[bank_b[bk]], writes=[V_b[kt]])
            mark("v")

            if seq_first:
                S.op("dve", lambda e: e.memset(carry[:, kt0, :], 0.0), writes=[carry_b])
            bk = pget()
            fns = []
            for tt in range(nt):
                fns += [MM(banks[bk][:, tt * 8:(tt + 1) * 8], xT[:, kc, tt * 128:(tt + 1) * 128], Wf[:, kc, :], kc == 0, kc == 7) for kc in range(8)]
            S.group("pe", fns, reads=[xT_b, wf_b], writes=[bank_b[bk]])
            zi = 0
            z = zt[zi]
            zb = zt_b[zi]
            zz = lambda r: z[:, r, 0:nt, :]
            S.op("dve", TT(zz(0), banks[bk][:, 0:nt * 8].rearrange("p (t h) -> p t h", h=8), bfb[:].unsqueeze(1).to_broadcast([128, nt, 8]), ALU.add),
                 reads=[bank_b[bk], cd_b], writes=[zb])
            S.op("dve", STT(zz(1), zz(0), -1.0, zz(0), ALU.mult, ALU.min), reads=[zb], writes=[zb])
            S.op("act", ACT(zz(1), zz(1), AF.Exp), reads=[zb], writes=[zb])
            S.op("act", ACT(zz(1), zz(1), AF.Ln, bias=cst[:, 0:1], scale=1.0), reads=[zb, const_b], writes=[zb])
            S.op("dve", STT(zz(2), zz(0), 0.0, zz(1), ALU.min, ALU.subtract), reads=[zb], writes=[zb])
            if nt == 1:
                S.dma("pool", DMA(lf_all, z[0:nreal, 2, 0, :]), zt_s[zi], reads=[zb])
            else:
                S.dma("pool", DMA(lf_all.rearrange("(t p) h -> p t h", p=128), z[:, 2, 0:nt, :]), zt_s[zi], reads=[zb])
            cumsum_tiles(kt0, nt, lambda i: z[:, 2, i, :], [zb])
            mark("f")
            bg(ck, 8)

            mark("bias")

        def mid_rest(ck):
            nt, nreal, kt0, seq_first = ck["nt"], ck["nreal"], ck["kt0"], ck["seq_first"]
            xrows = ck["xrows"]
            N = nt * 128
            x_rhs = lambda kc: xT[:, kc, 0:N]

            nkt = kt0 + nt
            accb = {}

            def begin_head(h):
                dqi = h % 2
                S.op("pool", TT(Dq[dqi][:, 0:nt, :], ident[:].unsqueeze(1).to_broadcast([128, nt, 128]),
                                negc[:, kt0:kt0 + nt, h:h + 1].to_broadcast([128, nt, 128]), ALU.mult),
                     reads=[negc_b, const_b], writes=[Dq_b[dqi]])
                accb[h] = pget(hold=True)

            def rec_S(h, kt):
                pr = h // 2
                dqi = h % 2
                dqv = Dq[dqi][:].rearrange("p j q -> p (j q)")
                c0 = max(0, kt - kt0) * 128
                bk = pget()
                fns = [MM(banks[bk][:, c0:N], KT[:, pr, kt * 128:(kt + 1) * 128], Qpad[:, h, c0:N], True, False),
                       MM(banks[bk][:, c0:N], negones[:], dqv[:, c0:N], False, kt < kt0)]
                if kt >= kt0:
                    fns.append(MM(banks[bk][:, c0:c0 + 128], ident[:], maskt[:], False, True))
                S.group("pe", fns, reads=[KT_b[kt], Q_b[h], const_b, Dq_b[dqi]], writes=[bank_b[bk]])
                return bk, c0

            def rec_E(h, kt, bk, c0):
                pi = nxt("pt", NPT)
                S.op("act", ACT(PT[pi][:, c0:N], banks[bk][:, c0:N], AF.Exp, bias=negc[:, kt, h:h + 1], scale=1.0),
                     reads=[bank_b[bk], negc_b], writes=[PT_b[pi]])
                return pi

            def rec_PV(h, kt, pi, c0):
                pr, odd = h // 2, h % 2
                acc = accb[h]
                vsl = slice(64, 192) if odd else slice(0, 128)
                fn = [MM(banks[acc][:, c0:N], Vaug[:, kt, pr, vsl], PT[pi][:, c0:N], kt == 0, kt == nkt - 1)]
                if kt == 0:
                    S.group("pe", fn, reads=[V_b[kt], PT_b[pi]], writes=[bank_b[acc]])
                else:
                    S.group("pe", fn, reads=[V_b[kt], PT_b[pi]], acc=[bank_b[acc]])

            def normalise(h):
                pr, odd = h // 2, h % 2
                acc = accb[h]
                lo, hi = (64, 128) if odd else (0, 64)
                dlo, dhi = (0, 64) if odd else (64, 128)
                S.op("act", ACT(den[0][lo:hi, 0:N], banks[acc][dlo:dhi, 0:N], AF.Copy), reads=[bank_b[acc]], writes=[den_b[0]])
                S.op("dve", lambda e: e.reciprocal(out=den[0][lo:hi, 0:N], in_=den[0][lo:hi, 0:N]), reads=[den_b[0]], writes=[den_b[0]])
                S.op("dve", TT(OT[lo:hi, pr, 0:N], banks[acc][lo:hi, 0:N], den[0][lo:hi, 0:N], ALU.mult),
                     reads=[bank_b[acc], den_b[0], OT_b[pr]], writes=[OT_b[pr]])
                prel(acc)

            items = [(h, kt) for h in range(8) for kt in range(nkt)]
            bgn = max(1, -(-80 // len(items)))
            norm_q = []
            DEPTH = 2
            pend = []
            nxt_i = 0
            for i, (h, kt) in enumerate(items):
                while nxt_i < len(items) and nxt_i <= i + DEPTH:
                    h2, kt2 = items[nxt_i]
                    if kt2 == 0:
                        begin_head(h2)
                    pend.append(rec_S(h2, kt2))
                    nxt_i += 1
                cur = pend.pop(0)
                pi = rec_E(h, kt, cur[0], cur[1])
                rec_PV(h, kt, pi, cur[1])
                if kt == nkt - 1:
                    norm_q.append([h, 2])
                for nq in list(norm_q):
                    if nq[1] == 0:
                        normalise(nq[0])
                        norm_q.remove(nq)
                    else:
                        nq[1] -= 1
                bg(ck, bgn)
                ln2_step(1)
            for nq in norm_q:
                normalise(nq[0])
            ln2_step(100)
            flush_stores()
            bg(ck, 100000)
            mark("attn")
            mark("convln")

            sco = sao = None
            for ep in range(4):
                if ep % 2 == 0:
                    sco = wload(("co", ep // 2))
                    sao = wload(("ao", ep // 2))
                sgc = wload(("gc", ep))
                sga = wload(("ga", ep))
                for el in range(2):
                    e_ = 2 * ep + el
                    b1 = pget()
                    fm_group(N, b1, sco, e_ % 4, 4, lambda kc: zT[:, kc, 0:N], zT_b)
                    b2 = pget()
                    fm_group(N, b2, sgc, el, 8, x_rhs, [xT_b])
                    a1 = nxt("A", NA)
                    S.op("act", ACT(At[a1][:, 0:N], banks[b2][:, 0:N], AF.Sigmoid), reads=[bank_b[b2]], writes=[At_b[a1]])
                    S.op("dve", TT(At[a1][:, 0:N], At[a1][:, 0:N], banks[b1][:, 0:N], ALU.mult), reads=[bank_b[b1], At_b[a1]], writes=[At_b[a1]])
                    b3 = pget()
                    fm_group(N, b3, sao, e_ % 4, 4, lambda kc: OT[:, kc, 0:N], OT_b)
                    b4 = pget()
                    fm_group(N, b4, sga, el, 8, x_rhs, [xT_b])
                    a2 = nxt("A", NA)
                    S.op("act", ACT(At[a2][:, 0:N], banks[b4][:, 0:N], AF.Sigmoid), reads=[bank_b[b4]], writes=[At_b[a2]])
                    S.op("dve", TT(At[a2][:, 0:N], At[a2][:, 0:N], banks[b3][:, 0:N], ALU.mult), reads=[bank_b[b3], At_b[a2]], writes=[At_b[a2]])
                    S.op("pool", TT(mT[:, e_, 0:N], At[a1][:, 0:N], At[a2][:, 0:N], ALU.add), reads=[At_b[a1], At_b[a2]], writes=MX[e_][0:nt])
            mark("gates")

            for tt in range(nt):
                S.dma("pool", DMA(Bt[0:nreal, tt, :], xrows(tt)), B_s[tt], writes=[B_b[tt]])
            so = [[wload(("o", eh, 0)), wload(("o", eh, 1))] for eh in range(2)]

            for tt in range(nt):
                for eh in range(2):
                    bk = pget()
                    S.group("pe", [MM(banks[bk][:, 0:512], mT[:, kc, tt * 128:(tt + 1) * 128], rT(so[eh][kc // 4])[:, kc % 4, :], kc == 0, kc == 7)
                                   for kc in range(8)], reads=[MX[kc][tt] for kc in range(8)] + [ring_b[so[eh][0]], ring_b[so[eh][1]]], writes=[bank_b[bk]])
                    S.op("dve", STT(Bt[:, tt, eh * 512:(eh + 1) * 512], Bt[:, tt, eh * 512:(eh + 1) * 512], ALPHA, banks[bk][:, 0:512], ALU.mult, ALU.add),
                         reads=[bank_b[bk], B_b[tt]], writes=[B_b[tt]])
            mark("ln1")

        def ln1_stats(ck):
            ck["ln1"] = []
            for tt in range(ck["nt"]):
                li = tt
                st, mv, lb = lnst4[li], lnmv4[li], ln4_b[li]
                S.op("dve", lambda e, st=st, tt=tt: e.bn_stats(out=st[:, 0, :], in_=Bt[:, tt, 0:512]), reads=[B_b[tt]], writes=[lb])
                S.op("dve", lambda e, st=st, tt=tt: e.bn_stats(out=st[:, 1, :], in_=Bt[:, tt, 512:1024]), reads=[B_b[tt], lb], writes=[lb])
                S.op("dve", lambda e, st=st, mv=mv: e.bn_aggr(out=mv[:, 0:2], in_=st[:]), reads=[lb], writes=[lb])
            for tt in range(ck["nt"]):
                mv, lb = lnmv4[tt], ln4_b[tt]
                S.op("act", ACT(mv[:, 2:3], mv[:, 1:2], AF.Ln, bias=cst[:, 1:2], scale=1.0), reads=[lb, const_b], writes=[lb])
                S.op("act", ACT(mv[:, 3:4], mv[:, 2:3], AF.Exp, scale=-0.5), reads=[lb], writes=[lb])

        def ln1_norm(ck):
            for tt in range(ck["nt"]):
                mv, lb = lnmv4[tt], ln4_b[tt]
                S.op("dve", TS(Bt[:, tt, :], Bt[:, tt, :], mv[:, 0:1], mv[:, 3:4], ALU.subtract, ALU.mult), reads=[lb, B_b[tt]], writes=[B_b[tt]])
                ln_affine(tt, 0)

        def ln1_cast(ck):
            for tt in range(ck["nt"]):
                S.op("act", ACT(Xb[:, tt, :], Bt[:, tt, :], AF.Copy), reads=[B_b[tt]], writes=[Xb_b[tt]])

        def ln1_tr(ck):
            for tt in range(ck["nt"]):
                bk = pget()
                bv = banks[bk][:].bitcast(BF16).rearrange("p (k t) -> p k t", k=8)
                S.group("pe", [TR(bv[:, kc, :], Xb[:, tt, kc * 128:(kc + 1) * 128]) for kc in range(8)],
                        reads=[Xb_b[tt], const_b], writes=[bank_b[bk]])
                S.op("act", ACT(mT[:, :, tt * 128:(tt + 1) * 128], bv, AF.Copy), reads=[bank_b[bk]], writes=[MX[kc][tt] for kc in range(8)])

        def back(ck, ck_bg):
            nt, nreal, y_o = ck["nt"], ck["nreal"], ck["y_o"]
            N = nt * 128
            x1_rhs = lambda kc: mT[:, kc, 0:N]
            x1_bufs = [MX[kc][t] for kc in range(8) for t in range(nt)]

            for fh in range(2):
                for i in range(8):
                    su = wload(("up", fh * 8 + i))
                    for el in range(2):
                        fl = 2 * i + el
                        bk = pget()
                        fm_group(N, bk, su, el, 8, x1_rhs, x1_bufs)
                        bg(ck_bg, 2)
                        ai = nxt("A", NA)
                        S.op("act", ACT(At[ai][:, 0:N], banks[bk][:, 0:N], AF.Relu), reads=[bank_b[bk]], writes=[At_b[ai]])
                        S.op("pool", TT(hT[:, fl, 0:N], At[ai][:, 0:N], At[ai][:, 0:N], ALU.mult), reads=[At_b[ai]], writes=[hT_b[fl]])
                for eh in range(2):
                    dbk = [pget(hold=True) for _ in range(nt)]
                    for g in range(4):
                        sd = wload(("dn", eh, fh * 4 + g))
                        for tt in range(nt):
                            fns = [MM(banks[dbk[tt]][:, 0:512], hT[:, g * 4 + kl, tt * 128:(tt + 1) * 128], rT(sd)[:, kl, :],
                                      g == 0 and kl == 0, g == 3 and kl == 3) for kl in range(4)]
                            rds = [ring_b[sd]] + hT_b[g * 4:g * 4 + 4]
                            if g == 0:
                                S.group("pe", fns, reads=rds, writes=[bank_b[dbk[tt]]])
                            else:
                                S.group("pe", fns, reads=rds, acc=[bank_b[dbk[tt]]])
                        bg(ck_bg, 3)
                    for tt in range(nt):
                        sl = slice(eh * 512, (eh + 1) * 512)
                        if fh == 0:
                            S.op("dve", STT(Bt[:, tt, sl], Bt[:, tt, sl], ALPHA, banks[dbk[tt]][:, 0:512], ALU.mult, ALU.add),
                                 reads=[bank_b[dbk[tt]], B_b[tt]], writes=[B_b[tt]])
                        else:
                            S.op("dve", TT(Bt[:, tt, sl], Bt[:, tt, sl], banks[dbk[tt]][:, 0:512], ALU.add),
                                 reads=[bank_b[dbk[tt]], B_b[tt]], writes=[B_b[tt]])
                        prel(dbk[tt])
            mark("ffn")
            def ln2_gen():
                for tt in range(nt):
                    for _ in ln_stats_norm_staged(tt):
                        yield
                    ln_affine(tt, 2)
                    pending.append((y_o(tt), Bt[0:nreal, tt, :], [B_b[tt]], Bst_s[tt]))
                    yield

            ln2_task[0] = ln2_gen()
            mark("ln2")

        chunks = [dict(nt=1, nreal=DEC, kt0=8, seq_first=False, conv_out=conv_s, xsrc=xs[0:DEC, :],
                       xrows=lambda tt: xs[0:DEC, :], y_o=lambda tt: y_s[0:DEC, :], k_o=lambda tt: k_s[0:DEC, :],
                       v_o=lambda tt: v_s[0:DEC, :], lf_all=lf_s[0:DEC, :], seq=None)]
        for s_ in range(nseq):
            for c in range(SEQ // CH):
                r0 = s_ * SEQ + c * CH
                rows = lambda tt, r0=r0: slice(r0 + tt * 128, r0 + (tt + 1) * 128)
                chunks.append(dict(nt=4, nreal=128, kt0=4 * c, seq_first=(c == 0),
                                   conv_out=conv_p[s_] if c == SEQ // CH - 1 else None,
                                   xsrc=xp[r0:r0 + CH, :],
                                   xrows=lambda tt, rows=rows: xp[rows(tt), :], y_o=lambda tt, rows=rows: y_p[rows(tt), :],
                                   k_o=lambda tt, rows=rows: k_p[rows(tt), :], v_o=lambda tt, rows=rows: v_p[rows(tt), :],
                                   lf_all=lf_p[r0:r0 + CH, :], seq=s_))

        for e_ in range(4):
            S.group("sp", [DMA(ubuf[:, e_, 0:HIST], cconv[:, e_ * 128:(e_ + 1) * 128].rearrange("t p -> p t"))], sem=hist_s, amount=16, writes=[])
        for e_ in range(4):
            u_b[e_].lw = (hist_s, S.cnt[hist_s])
        S.op("dve", lambda e: e.memset(carry[:, 0, :], 0.0), writes=[carry_b])
        prefetch_x(chunks[0])
        ckv = hT[:, 0:8, :]
        S.dma("pool", DMA(ckv, ck.rearrange("(k p) e -> p k e", p=128)), ck_s, writes=hT_b[0:8])
        cvv = cv.rearrange("(k p) (r a d) -> p k r a d", p=128, r=4, a=2)
        for kt in range(8):
            S.group("pool", [DMA(Vaug[:, kt, :, 0:64], cvv[:, kt, :, 0, :])], sem=cv_s, amount=16)
            S.group("pool", [DMA(Vaug[:, kt, :, 128:192], cvv[:, kt, :, 1, :])], sem=cv_s, amount=16)
        for kt in range(8):
            V_b[kt].lw = (cv_s, S.cnt[cv_s])
        lfc_t = cacc[:, 0, 0:64].rearrange("p (k h) -> p k h", h=8)
        lfc_b = cacc_b[0]
        S.dma("sp", DMA(lfc_t, clf.rearrange("(k p) h -> p k h", p=128)), ck_s2, writes=[lfc_b])
        for kt in range(8):
            bk2 = pget()
            bv = banks[bk2][:].bitcast(BF16)[:, 0:512].rearrange("p (k t) -> p k t", k=4)
            S.group("pe", [TR(bv[:, pr, :], ckv[:, kt, pr * 128:(pr + 1) * 128]) for pr in range(4)],
                    reads=[hT_b[kt], const_b], writes=[bank_b[bk2]])
            S.op("dve", CP(KT[:, :, kt * 128:(kt + 1) * 128], bv), reads=[bank_b[bk2]], writes=[KT_b[kt]])
        cumsum_tiles(0, 8, lambda i: lfc_t[:, i, :], [lfc_b])

        wgrp = [(0, 18), (18, 26), (26, 42), (42, NU)]
        wsc_gs = [S.new_sem(f"wsc{g}") for g in range(len(wgrp))]
        wsc_gb = [Buf(f"wsc{g}") for g in range(len(wgrp))]
        ugrp = {}
        for g, (u0, u1) in enumerate(wgrp):
            for u in range(u0, u1):
                wn, r0, nr, c0, ncol = ULIST[u]
                kcn = nr // 128
                src = W[wn][r0:r0 + nr, c0:c0 + ncol].rearrange("(kc p) e -> p kc e", p=128)
                dst = wsc[u].rearrange("p (kc e) -> p kc e", kc=kcn)
                S.group("pool", [DMA(dst, src)], sem=wsc_gs[g], amount=16)
                ugrp[u] = g
            wsc_gb[g].lw = (wsc_gs[g], S.cnt[wsc_gs[g]])

        try:
            front_a(chunks[0])
            prefetch_x(chunks[1] if len(chunks) > 1 else None)
            front_b(chunks[0])
            mid_qkvf(chunks[0])
            for ci, ckd in enumerate(chunks):
                mid_rest(ckd)
                nxt_ck = chunks[ci + 1] if ci + 1 < len(chunks) else None
                if nxt_ck is not None:
                    front_a(nxt_ck)
                ln1_stats(ckd)
                if nxt_ck is not None:
                    front_b(nxt_ck)
                ln1_norm(ckd)
                if nxt_ck is not None:
                    mid_qkvf(nxt_ck)
                ln1_cast(ckd)
                ln1_tr(ckd)
                prefetch_x(chunks[ci + 2] if ci + 2 < len(chunks) else None)
                back(ckd, nxt_ck)
        except _Stop:
            pass
        ln2_step(100)
        flush_stores()

        if dbg_o is not None:
            dsm = S.new_sem("dbgs")
            S.dma("pool", DMA(dbg_o[:, 0:144], carry[:].rearrange("p a h -> p (a h)")), dsm, reads=[carry_b])
            S.dma("pool", DMA(dbg_o[:, 144:280], negc[:].rearrange("p a h -> p (a h)")), dsm, reads=[negc_b])
            S.final_wait("pool", [(dsm, 32)])
        mark("END")
        if _PHASES is not None:
            _PHASES.extend(_marks)
        fin = [(zs, S.cnt[zs]) for zs in zt_s + kvst_s + Bst_s + cvo_s]
        S.final_wait("pool", fin)
        S.emit()
    return nc


_CACHE = {}


def kernel(x_prompt, x_sample, cache_conv, cache_k, cache_v, cache_logf,
           w_in, b_f, w_dw, b_dw, ln_conv_g, ln_conv_b, w_conv_out, w_attn_out, w_o,
           ln1_g, ln1_b, w_up, w_down, ln2_g, ln2_b, _dbg=None, _nseq=NSEQ, _ncores=NCORES, _raw=False):
    f = lambda a: np.ascontiguousarray(np.asarray(a, dtype=np.float32))
    x_prompt, x_sample = f(x_prompt), f(x_sample)
    B = x_prompt.shape[0]
    nc = build_program(_dbg, _nseq)
    wdw = f(np.asarray(w_dw)[0].T.reshape(4, 128, 31).transpose(1, 0, 2))
    cpar = f(np.stack([np.asarray(b_dw)[0], np.asarray(ln_conv_g)[0], np.asarray(ln_conv_b)[0]]).reshape(3, 4, 128).transpose(2, 1, 0))
    lnp = f(np.stack([np.asarray(ln1_g)[0], np.asarray(ln1_b)[0], np.asarray(ln2_g)[0], np.asarray(ln2_b)[0]]))
    shared = {
        "w_in": f(np.asarray(w_in)[0]), "w_co": f(np.asarray(w_conv_out)[0]), "w_ao": f(np.asarray(w_attn_out)[0]),
        "w_o": f(np.asarray(w_o)[0]), "w_up": f(np.asarray(w_up)[0]), "w_down": f(np.asarray(w_down)[0]),
        "b_f": f(np.asarray(b_f)[0:1]), "wdw": wdw, "cpar": cpar, "lnp": lnp,
    }
    cache_conv, cache_k, cache_v, cache_logf = f(cache_conv), f(cache_k), f(cache_v), f(cache_logf)
    in_maps = []
    for c in range(_ncores):
        m = dict(shared)
        m["xp"] = x_prompt[NSEQ * c:NSEQ * c + _nseq].reshape(_nseq * SEQ, D)
        m["xs"] = x_sample[c]
        m["cconv"] = cache_conv[0, c]
        m["ck"] = cache_k[0, c].reshape(PAST, 512)
        m["cv"] = cache_v[0, c].reshape(PAST, 512)
        m["clf"] = cache_logf[0, c]
        in_maps.append(m)
    res = run_bass_kernel_spmd(nc, in_maps, core_ids=list(range(_ncores)))
    if _raw:
        return res.results
    R = res.results
    cat = lambda k: np.concatenate([np.asarray(r[k]) for r in R], axis=0)
    y_prompt = cat("y_p").reshape(B, SEQ, D)
    y_sample = np.stack([np.asarray(r["y_s"]) for r in R]).reshape(NCORES, DEC, D)
    conv_prompt = cat("conv_p").reshape(1, B, HIST, 512)
    k_prompt = cat("k_p").reshape(1, B, SEQ, 8, 64)
    v_prompt = cat("v_p").reshape(1, B, SEQ, 8, 64)
    logf_prompt = cat("lf_p").reshape(1, B, SEQ, 8)
    conv_sample = np.stack([np.asarray(r["conv_s"]) for r in R]).reshape(1, NCORES, HIST, 512)
    k_sample = np.stack([np.asarray(r["k_s"]) for r in R]).reshape(1, NCORES, DEC, 8, 64)
    v_sample = np.stack([np.asarray(r["v_s"]) for r in R]).reshape(1, NCORES, DEC, 8, 64)
    logf_sample = np.stack([np.asarray(r["lf_s"]) for r in R]).reshape(1, NCORES, DEC, 8)
    return (y_prompt, y_sample, conv_prompt, k_prompt, v_prompt, logf_prompt,
            conv_sample, k_sample, v_sample, logf_sample)
```
